# Optimizing a Trainium2 kernel written in Bass

```python
import jax, jax.numpy as jnp
from jax import lax
import numpy as np

D_MODEL = 1024
BATCH = 8
SEQ = 4096
DEPTH = 4

CTX_LEN = 256
GRID_W = 64

F_GROUPS = 4
F_GROUP_DIM = D_MODEL // 16
F_WIDTH = F_GROUPS * F_GROUP_DIM
G_HEADS = 4
G_HEAD_DIM = D_MODEL // 16
G_WIDTH = G_HEADS * G_HEAD_DIM
CHUNK = 128
A_HEADS = 8
A_NOPE = D_MODEL // 16
A_ROPE = D_MODEL // 32
A_V = D_MODEL // 16
A_QK = A_NOPE + A_ROPE
A_WIDTH = A_HEADS * A_V
Q_LORA = D_MODEL // 4
KV_LORA = D_MODEL // 8

MIX_WIDTH = F_WIDTH + G_WIDTH + A_WIDTH
IN_WIDTH = 2 * F_WIDTH + 3 * G_WIDTH + Q_LORA + KV_LORA + A_ROPE + A_WIDTH
ROPE_BASE = 10000.0
EPS = 1e-6
Q_BLOCK = 128

kernel_name = "hybrid_fourier_gmlp_mla_prefix_dit"


def _rmsnorm(t, g):
    tf = t.astype(jnp.float32)
    tf = tf * lax.rsqrt(jnp.mean(tf * tf, axis=-1, keepdims=True) + EPS)
    return (tf * g.astype(jnp.float32)).astype(t.dtype)


def _layernorm(t, g):
    tf = t.astype(jnp.float32)
    tf = tf - jnp.mean(tf, axis=-1, keepdims=True)
    tf = tf * lax.rsqrt(jnp.mean(tf * tf, axis=-1, keepdims=True) + EPS)
    return (tf * g.astype(jnp.float32)).astype(t.dtype)


def _split_in(p):
    sizes = (F_WIDTH, F_WIDTH, G_WIDTH, G_WIDTH, G_WIDTH, Q_LORA, KV_LORA, A_ROPE, A_WIDTH)
    parts = []
    start = 0
    for s in sizes:
        parts.append(p[..., start:start + s])
        start += s
    return parts


def _axial_rope(n):
    rows = n // GRID_W
    row = jnp.repeat(jnp.arange(rows, dtype=jnp.float32), GRID_W)
    col = jnp.tile(jnp.arange(GRID_W, dtype=jnp.float32), rows)
    half = A_ROPE // 2
    inv = ROPE_BASE ** (-jnp.arange(0, half, 2, dtype=jnp.float32) / half)
    ang_r = row[:, None] * inv[None, :]
    ang_c = col[:, None] * inv[None, :]
    ang = jnp.concatenate([ang_r, ang_r, ang_c, ang_c], axis=-1)
    return jnp.cos(ang), jnp.sin(ang)


def _apply_rope(t, rope):
    cos, sin = rope
    q = A_ROPE // 4
    x1, x2, x3, x4 = t[..., :q], t[..., q:2 * q], t[..., 2 * q:3 * q], t[..., 3 * q:]
    rot = jnp.concatenate([-x2, x1, -x4, x3], axis=-1)
    cos = cos[:, None, :].astype(t.dtype)
    sin = sin[:, None, :].astype(t.dtype)
    return t * cos + rot * sin


def _fourier(u):
    b, n, _ = u.shape
    ug = u.astype(jnp.float32).reshape(b, n, F_GROUPS, F_GROUP_DIM).transpose(0, 2, 1, 3)
    f = jnp.fft.fft2(ug, norm="ortho").real
    return f.transpose(0, 2, 1, 3).reshape(b, n, F_WIDTH).astype(u.dtype)


def _spatial_gate(u, v, ln_g, ws, bs):
    b, n, _ = v.shape
    vh = v.reshape(b, n // CHUNK, CHUNK, G_HEADS, G_HEAD_DIM)
    vh = _layernorm(vh, ln_g)
    mixed = jnp.einsum('hpq,bcqhd->bcphd', ws, vh) + bs.T[None, None, :, :, None]
    return u * mixed.reshape(b, n, G_WIDTH)


def _mla_qkv(c_q, c_kv, k_rope, q_a_g, w_uq, kv_a_g, w_ukv, q_norm_g, k_norm_g, rope):
    b, n, _ = c_q.shape
    q = (_rmsnorm(c_q, q_a_g) @ w_uq).reshape(b, n, A_HEADS, A_QK)
    kv = (_rmsnorm(c_kv, kv_a_g) @ w_ukv).reshape(b, n, A_HEADS, A_NOPE + A_V)
    k_nope, v = kv[..., :A_NOPE], kv[..., A_NOPE:]
    k_r = jnp.broadcast_to(k_rope[:, :, None, :], (b, n, A_HEADS, A_ROPE))
    k = jnp.concatenate([k_nope, k_r], axis=-1)
    q = _rmsnorm(q, q_norm_g)
    k = _rmsnorm(k, k_norm_g)
    if rope is not None:
        q = jnp.concatenate([q[..., :A_NOPE], _apply_rope(q[..., A_NOPE:], rope)], axis=-1)
        k = jnp.concatenate([k[..., :A_NOPE], _apply_rope(k[..., A_NOPE:], rope)], axis=-1)
    return q, k, v


def _branches(h, w_in, w_fmix, g_ln_g, g_ws, g_bs, q_a_g, w_uq, kv_a_g, w_ukv, q_norm_g, k_norm_g, rope):
    f_in, f_gate, g_u, g_v, g_gate, c_q, c_kv, k_rope, a_gate = _split_in(h @ w_in)
    f_out = (_fourier(f_in) @ w_fmix) * jax.nn.silu(f_gate)
    g_out = _spatial_gate(g_u, g_v, g_ln_g, g_ws, g_bs) * jax.nn.silu(g_gate)
    q, k, v = _mla_qkv(c_q, c_kv, k_rope, q_a_g, w_uq, kv_a_g, w_ukv, q_norm_g, k_norm_g, rope)
    return f_out, g_out, q, k, v, a_gate


def _attend(q, k, v):
    s = jnp.einsum('bqhd,bkhd->bhqk', q, k).astype(jnp.float32) * (A_QK ** -0.5)
    p = jax.nn.softmax(s, axis=-1).astype(v.dtype)
    return jnp.einsum('bhqk,bkhd->bqhd', p, v)


def _blocked_attention(q, k, v):
    b, n, h, d = q.shape
    nb = n // Q_BLOCK
    qb = q.reshape(b, nb, Q_BLOCK, h, d).transpose(1, 0, 2, 3, 4)
    o = lax.map(lambda blk: _attend(blk, k, v), qb)
    return o.transpose(1, 0, 2, 3, 4).reshape(b, n, h * A_V)


def setup_inputs(seed: int = 0) -> dict:
    key = jax.random.key(seed)
    ks = jax.random.split(key, 24)
    f32 = jnp.float32
    nrm = lambda k, shape, s: jax.random.normal(k, shape, f32) * s
    gain = lambda k, shape: 1.0 + 0.02 * jax.random.normal(k, shape, f32)
    return {
        "x": nrm(ks[0], (BATCH, SEQ, D_MODEL), 1.0),
        "c": nrm(ks[1], (BATCH, D_MODEL), 1.0),
        "ctx": nrm(ks[2], (BATCH, CTX_LEN, D_MODEL), 1.0),
        "c_ctx": nrm(ks[3], (D_MODEL,), 1.0),
        "w_mod": nrm(ks[4], (DEPTH, D_MODEL, 3 * D_MODEL), 0.5 * D_MODEL ** -0.5),
        "b_mod": nrm(ks[5], (DEPTH, 3 * D_MODEL), 0.01),
        "norm_g": gain(ks[6], (DEPTH, D_MODEL)),
        "w_in": nrm(ks[7], (DEPTH, D_MODEL, IN_WIDTH), D_MODEL ** -0.5),
        "w_fmix": nrm(ks[8], (DEPTH, F_WIDTH, F_WIDTH), F_WIDTH ** -0.5),
        "g_ln_g": gain(ks[9], (DEPTH, G_HEAD_DIM)),
        "g_ws": nrm(ks[10], (DEPTH, G_HEADS, CHUNK, CHUNK), CHUNK ** -0.5),
        "g_bs": 1.0 + nrm(ks[11], (DEPTH, G_HEADS, CHUNK), 0.01),
        "q_a_g": gain(ks[12], (DEPTH, Q_LORA)),
        "w_uq": nrm(ks[13], (DEPTH, Q_LORA, A_HEADS * A_QK), Q_LORA ** -0.5),
        "kv_a_g": gain(ks[14], (DEPTH, KV_LORA)),
        "w_ukv": nrm(ks[15], (DEPTH, KV_LORA, A_HEADS * (A_NOPE + A_V)), KV_LORA ** -0.5),
        "q_norm_g": gain(ks[16], (DEPTH, A_QK)),
        "k_norm_g": gain(ks[17], (DEPTH, A_QK)),
        "w_out": nrm(ks[18], (DEPTH, MIX_WIDTH, D_MODEL), MIX_WIDTH ** -0.5),
    }


def reference(x, c, ctx, c_ctx, w_mod, b_mod, norm_g, w_in, w_fmix, g_ln_g, g_ws, g_bs,
              q_a_g, w_uq, kv_a_g, w_ukv, q_norm_g, k_norm_g, w_out):
    n = x.shape[1]
    rope = _axial_rope(n)
    y = ctx
    silu_c = jax.nn.silu(c)
    silu_cc = jax.nn.silu(c_ctx)
    for l in range(DEPTH):
        shift, scale, gate = jnp.split(silu_c @ w_mod[l] + b_mod[l], 3, axis=-1)
        shift_c, scale_c, gate_c = jnp.split(silu_cc @ w_mod[l] + b_mod[l], 3, axis=-1)
        hx = _rmsnorm(x, norm_g[l]) * (1.0 + scale[:, None, :]) + shift[:, None, :]
        hy = _rmsnorm(y, norm_g[l]) * (1.0 + scale_c) + shift_c
        params = (w_in[l], w_fmix[l], g_ln_g[l], g_ws[l], g_bs[l], q_a_g[l], w_uq[l],
                  kv_a_g[l], w_ukv[l], q_norm_g[l], k_norm_g[l])
        fy, gy, qy, ky, vy, ay = _branches(hy, *params, None)
        fx, gx, qx, kx, vx, ax = _branches(hx, *params, rope)
        k_all = jnp.concatenate([ky, kx], axis=1)
        v_all = jnp.concatenate([vy, vx], axis=1)
        ox = _blocked_attention(qx, k_all, v_all) * jax.nn.silu(ax)
        x = x + gate[:, None, :] * (jnp.concatenate([fx, gx, ox], axis=-1) @ w_out[l])
        if l < DEPTH - 1:
            b = y.shape[0]
            oy = _attend(qy, ky, vy).reshape(b, y.shape[1], A_WIDTH) * jax.nn.silu(ay)
            y = y + gate_c * (jnp.concatenate([fy, gy, oy], axis=-1) @ w_out[l])
    return x
```

```python
import contextlib
import numpy as np
import concourse.bass as bass
import concourse.mybir as mybir
from concourse.bass_utils import run_bass_kernel_spmd

F32 = mybir.dt.float32
BF16 = mybir.dt.bfloat16
AF = mybir.ActivationFunctionType
ALU = mybir.AluOpType
AX = mybir.AxisListType

DEPTH = 4
D = 1024
NX = 4096
NC_ = 256
T = NX + NC_
NT = T // 128
EPS = 1e-6
ENGS = ("pe", "act", "dve", "pool", "sp")


class _Instr:
    __slots__ = ("eng", "fn", "deps", "dma", "idx", "sig", "dsem", "dval")

    def __init__(self, eng, fn, deps, dma, idx):
        self.eng, self.fn, self.deps, self.dma, self.idx = eng, fn, deps, dma, idx
        self.sig = None
        self.dsem = None
        self.dval = None


class Prog:
    def __init__(self, nc, n_dma_sems=24):
        self.nc = nc
        self.instrs = []
        self.last_write = {}
        self.readers = {}
        self.n_dma_sems = n_dma_sems
        self.excl = set()
        self.since_bar_dma = []
        self.last_eng = {}

    def add(self, eng, fn, reads=(), writes=(), dma=False):
        idx = len(self.instrs)
        deps = set()
        instrs = self.instrs

        def same(o):
            p = instrs[o]
            return (not dma) and (not p.dma) and p.eng == eng and eng != "pool"

        xr = [b for b in reads if b in self.excl]
        if xr:
            reads = [b for b in reads if b not in self.excl]
            writes = list(writes) + [b for b in xr if b not in writes]
            for b in xr:
                w = self.last_write.get(b)
                if w is not None:
                    deps.add(w)
        for b in reads:
            w = self.last_write.get(b)
            if w is not None:
                deps.add(w)
        for b in writes:
            w = self.last_write.get(b)
            if w is not None and not same(w):
                deps.add(w)
            for r in self.readers.get(b, ()):
                if not same(r):
                    deps.add(r)
        ins = _Instr(eng, fn, deps, dma, idx)
        self.instrs.append(ins)
        for b in reads:
            self.readers.setdefault(b, []).append(idx)
        for b in writes:
            self.last_write[b] = idx
            self.readers[b] = []
        if dma:
            self.since_bar_dma.append(idx)
        else:
            self.last_eng[eng] = idx
        return idx

    def barrier(self):
        deps = set(self.last_eng.values()) | set(self.since_bar_dma)
        self.since_bar_dma = []
        for e in ENGS:
            idx = len(self.instrs)
            ins = _Instr(e, None, set(deps), False, idx)
            self.instrs.append(ins)

    def emit(self, final_wait_eng="sp"):
        nc = self.nc
        instrs = self.instrs
        needed = set()
        for ins in instrs:
            nd = set()
            for d in ins.deps:
                p = instrs[d]
                if p.eng == "pe" and ins.eng == "pe" and not p.dma and not ins.dma:
                    continue
                nd.add(d)
            ins.deps = nd
            needed.update(nd)
        LIM = 3000
        cnt = {e: 0 for e in ENGS}
        dcnt = [0] * self.n_dma_sems
        dlast = [None] * self.n_dma_sems
        NSP = 16
        nq = {"sp": 0, "pool": 0}
        for ins in instrs:
            if ins.dma:
                if ins.eng == "sp":
                    k = nq["sp"] % NSP
                else:
                    k = NSP + nq["pool"] % (self.n_dma_sems - NSP)
                nq[ins.eng] += 1
                prev = dlast[k]
                if prev is not None:
                    ins.deps.add(prev)
                dcnt[k] += 16
                ins.dsem, ins.dval = k, dcnt[k]
                dlast[k] = ins.idx
            elif ins.idx in needed and ins.fn is not None:
                n = cnt[ins.eng]
                cnt[ins.eng] += 1
                ins.sig = (n // LIM, n % LIM + 1)
        print('SEM COUNTS', cnt, 'dma', max(dcnt), 'ninstr', len(instrs), flush=True)
        with contextlib.ExitStack() as st:
            esem = {e: [st.enter_context(nc.semaphore("s_%s%d" % (e, i))) for i in range(cnt[e] // LIM + 1)] for e in ENGS}
            dsems = [st.enter_context(nc.semaphore("d%d" % i)) for i in range(self.n_dma_sems)]
            block = st.enter_context(nc.Block())
            per_eng = {e: [i for i in instrs if i.eng == e] for e in ENGS}
            final = {}
            for i in instrs:
                if i.dma:
                    final[i.dsem] = max(final.get(i.dsem, 0), i.dval)
            final_eng = {}
            for i in instrs:
                if i.sig is not None:
                    final_eng[i.eng] = i.sig
            nds = self.n_dma_sems

            def run(engname, eng):
                waited_e = {e: (0, 0) for e in ENGS}
                waited_d = [0] * nds
                for ins in per_eng[engname]:
                    need_e = {}
                    need_d = {}
                    for d in ins.deps:
                        p = instrs[d]
                        if p.dma:
                            if p.dval > waited_d[p.dsem]:
                                need_d[p.dsem] = max(need_d.get(p.dsem, 0), p.dval)
                        else:
                            if p.sig > waited_e[p.eng]:
                                need_e[p.eng] = max(need_e.get(p.eng, (0, 0)), p.sig)
                    for e, v in need_e.items():
                        eng.wait_ge(esem[e][v[0]], v[1])
                        waited_e[e] = v
                    for k, v in need_d.items():
                        eng.wait_ge(dsems[k], v)
                        waited_d[k] = v
                    if ins.fn is None:
                        continue
                    bi = ins.fn(eng)
                    if ins.dma:
                        bi.then_inc(dsems[ins.dsem], 16)
                    elif ins.sig is not None:
                        bi.then_inc(esem[ins.eng][ins.sig[0]], 1)
                if engname == final_wait_eng:
                    for k, v in final.items():
                        if v > waited_d[k]:
                            eng.wait_ge(dsems[k], v)
                    for e, v in final_eng.items():
                        if v > waited_e[e] and e != engname:
                            eng.wait_ge(esem[e][v[0]], v[1])

            @block.tensor
            def _(eng):
                run("pe", eng)

            @block.scalar
            def _(eng):
                run("act", eng)

            @block.vector
            def _(eng):
                run("dve", eng)

            @block.gpsimd
            def _(eng):
                run("pool", eng)

            @block.sync
            def _(eng):
                run("sp", eng)


_PERM = np.concatenate([np.arange(8, 16), np.arange(0, 8), np.arange(24, 32), np.arange(16, 24)])
_SIGN = np.concatenate([-np.ones(8), np.ones(8), -np.ones(8), np.ones(8)]).astype(np.float32)

V_NG, V_BM, V_QAG, V_KVAG, V_GQ, V_GKN, V_GKE = 0, 8, 32, 34, 35, 36, 37
NV = 38


def _consts():
    n = np.arange(NX)
    row = (n // 64).astype(np.float64)
    col = (n % 64).astype(np.float64)
    inv = 10000.0 ** (-np.arange(0, 16, 2, dtype=np.float64) / 16.0)
    ang = np.concatenate([row[:, None] * inv, row[:, None] * inv, col[:, None] * inv, col[:, None] * inv], -1)
    tab = np.zeros((64, T), np.float32)
    tab[0:32, :NC_] = 1.0
    tab[0:32, NC_:] = np.cos(ang).T
    tab[32:64, NC_:] = (np.sin(ang) * _SIGN[None, :]).T
    cc = np.arange(64)
    a64 = 2 * np.pi * np.outer(cc, cc) / 64.0
    C64, S64 = np.cos(a64), np.sin(a64)
    z = np.zeros((64, 64))
    bd = lambda m: np.block([[m, z], [z, m]])
    csbd = np.concatenate([bd(C64) / 8.0, -bd(S64) / 8.0], 1).astype(np.float32)
    c64s64 = np.concatenate([bd(C64) / 8.0, bd(S64) / 8.0], 1).astype(np.float32)
    tw = np.zeros((128, 32, 384), np.float32)
    r = np.arange(64)
    p = np.arange(64)
    for j in range(32):
        for clo in range(2):
            nn = 64 * r + 2 * j + clo
            a = 2 * np.pi * np.outer(nn, p) / NX
            sl = slice(clo * 64, clo * 64 + 64)
            tw[sl, j, 0 + clo * 64:0 + clo * 64 + 64] = np.sin(a) / 8.0
            tw[sl, j, 128 + clo * 64:128 + clo * 64 + 64] = np.cos(a) / 8.0
            tw[sl, j, 256 + clo * 64:256 + clo * 64 + 64] = -np.sin(a) / 8.0
    nn = np.arange(256)
    a256 = 2 * np.pi * np.outer(nn, nn) / 256.0
    c256 = np.concatenate([np.cos(a256) / 16.0, np.sin(a256) / 16.0], 1).astype(np.float32)
    c256 = c256.reshape(2, 128, 512).transpose(1, 0, 2).copy()
    return tab, csbd, c64s64, tw.reshape(128, 32 * 384), c256.reshape(128, 1024)


def _layout_weights(inp):
    f = np.float32
    w_in = inp["w_in"]
    kr = w_in[:, :, 1664:1696]
    w_in_ext = np.concatenate([w_in[:, :, :1696], kr[:, :, _PERM], w_in[:, :, 1696:]], axis=2).astype(f)
    w_uq = inp["w_uq"].reshape(DEPTH, 256, 8, 96)
    w_uq_ext = np.concatenate([w_uq[:, :, :, 64:96], w_uq[:, :, :, 64 + _PERM], w_uq[:, :, :, 0:64]], axis=3).reshape(DEPTH, 256, 1024).astype(f)
    wsT = np.ascontiguousarray(np.transpose(inp["g_ws"], (0, 1, 3, 2))).astype(f)
    vecs = np.zeros((DEPTH, 128, NV), f)
    for l in range(DEPTH):
        vecs[l, :, V_NG:V_NG + 8] = inp["norm_g"][l].reshape(8, 128).T
        vecs[l, :, V_BM:V_BM + 24] = inp["b_mod"][l].reshape(24, 128).T
        vecs[l, :, V_QAG:V_QAG + 2] = inp["q_a_g"][l].reshape(2, 128).T
        vecs[l, :, V_KVAG] = inp["kv_a_g"][l]
        qg = inp["q_norm_g"][l]
        kg = inp["k_norm_g"][l]
        vecs[l, :, V_GQ] = np.concatenate([qg[64:96], qg[64 + _PERM], qg[0:64]])
        vecs[l, 0:64, V_GKN] = kg[0:64]
        vecs[l, 64:128, V_GKN] = kg[0:64]
        vecs[l, 0:64, V_GKE] = np.concatenate([kg[64:96], kg[64 + _PERM]])
    glnbc = np.ascontiguousarray(np.broadcast_to(np.tile(inp["g_ln_g"], (1, 4))[:, None, :], (DEPTH, 128, 256))).astype(f)
    bs = inp["g_bs"]
    bsbc = np.ascontiguousarray(np.broadcast_to(bs[:, :, None, :], (DEPTH, 4, 64, 128))).reshape(DEPTH, 2, 128, 128)
    bsbc = np.ascontiguousarray(np.transpose(bsbc, (0, 2, 1, 3))).reshape(DEPTH, 128, 256).astype(f)
    return dict(w_in_ext=w_in_ext, w_uq_ext=w_uq_ext, wsT=wsT, vecs=vecs, glnbc=glnbc, bsbc=bsbc)


def build(depth=DEPTH, dbg=None, phases=4, sub=99):
    KSC = -0.5 * float(np.log(96.0))
    import os
    DBG_L = int(os.environ.get('DBG_L', '0'))
    nc = bass.Bass("TRN2", target_bir_lowering=False)
    P = Prog(nc)

    def din(name, shape, dt=F32):
        return nc.dram_tensor(name, list(shape), dt, kind="ExternalInput").ap()

    xin = din("xin", [T, D])
    cvec = din("cvec", [2, D])
    w_mod = din("w_mod", [DEPTH, D, 3 * D])
    w_in = din("w_in_ext", [DEPTH, D, 2240])
    w_fmix = din("w_fmix", [DEPTH, 256, 256])
    wsT_d = din("wsT", [DEPTH, 4, 128, 128])
    w_uq = din("w_uq_ext", [DEPTH, 256, 1024])
    w_ukv = din("w_ukv", [DEPTH, 128, 1024])
    w_out = din("w_out", [DEPTH, D, D])
    vecs_d = din("vecs", [DEPTH, 128, NV])
    glnbc_d = din("glnbc", [DEPTH, 128, 256])
    bsbc_d = din("bsbc", [DEPTH, 128, 256])
    tab_d = din("tab", [64, T])
    csbd_d = din("csbd", [128, 256])
    c64_d = din("c64s64", [128, 256])
    tw_d = din("tw", [128, 32 * 384])
    c256_d = din("c256", [128, 1024])
    out = nc.dram_tensor("out", [NX, D], F32, kind="ExternalOutput").ap()
    bufA = nc.dram_tensor("bufA", [T, D], F32, kind="Internal").ap()
    bufB = nc.dram_tensor("bufB", [T, D], F32, kind="Internal").ap()
    Kscr = nc.dram_tensor("Kscr", [8, 96, T], BF16, kind="Internal").ap()
    Vscr = nc.dram_tensor("Vscr", [8, 128, NT * 128], BF16, kind="Internal").ap()
    gscr = nc.dram_tensor("gscr", [2, D], F32, kind="Internal").ap()
    dbg_out = {}
    if dbg:
        for name, shape in dbg.items():
            dbg_out[name] = nc.dram_tensor("dbg_" + name, list(shape), F32, kind="ExternalOutput").ap()

    def sb(name, shape, dt=F32):
        return nc.alloc_sbuf_tensor("sb_" + name, list(shape), dt)

    ident = sb("ident", [128, 128], BF16)
    ones_bf = sb("ones_bf", [128, 128], BF16)
    onesq = sb("onesq", [128, 128], BF16)
    tab = sb("tab", [64, T], BF16)
    csbd = sb("csbd", [128, 256], BF16)
    c64 = sb("c64", [128, 256], BF16)
    c256 = sb("c256", [128, 2, 512], BF16)
    sk = sb("sk", [128, NT, 8])
    gate_bc = sb("gate_bc", [128, 2, D])
    sc = sb("sc", [128, 8, 2])
    scb = sb("scb", [128, 8, 2], BF16)
    cv = sb("cv", [128, 8, 2])
    modv = sb("modv", [128, 24, 2])
    amod = sb("amod", [128, 8, 2])
    vecs = sb("vecs", [128, NV])
    glnbc = sb("glnbc", [128, 256])
    bsbc = sb("bsbc", [128, 2, 128])
    UT = sb("UT", [128, 2, T], BF16)
    w_in_sb = sb("w_in_sb", [128, 8, 2240], BF16)
    w_out_sb = sb("w_out_sb", [128, 8, D], BF16)
    wq_sb = sb("wq_sb", [128, 2, 1024], BF16)
    wukv_sb = sb("wukv_sb", [128, 1024], BF16)
    wfm_sb = sb("wfm_sb", [128, 2, 256], BF16)
    wsT_sb = sb("wsT_sb", [128, 4, 128], BF16)
    xt = sb("xt", [128, 4, D])
    tq1 = sb("tq1", [128, 512])
    rbq1 = sb("rbq1", [128, 512])
    xs = [sb("xs%d" % i, [128, D], BF16) for i in range(2)]
    ssq = sb("ssq", [128, 4])
    rstd = sb("rstd", [128, 4])
    hT = sb("hT", [128, 8, 512], BF16)
    tmpA = sb("tmpA", [128, 512])
    tmpB = sb("tmpB", [128, 512])
    tmpC = sb("tmpC", [128, 512])
    tmpD = sb("tmpD", [128, 512])
    sqb = [sb("sqb%d" % i, [128, 512], BF16) for i in range(2)]
    rbc = sb("rbc", [128, 512])

    ARENA = 62 * 1024
    arena_t = sb("arena", [128, ARENA // 4])
    aoff = [0]

    def areset():
        aoff[0] = 0

    def carve(shape, dt=F32):
        esz = 4 if dt == F32 else 2
        n = 1
        for d_ in shape[1:]:
            n *= d_
        nbytes = (n * esz + 31) // 32 * 32
        assert aoff[0] + nbytes <= ARENA, ("arena overflow", aoff[0], nbytes)
        a = arena_t[0:shape[0], aoff[0] // 4:(aoff[0] + nbytes) // 4]
        aoff[0] += nbytes
        if dt == BF16:
            a = a.bitcast(BF16)
        a = a[:, 0:n]
        if len(shape) == 3:
            a = a.rearrange("p (a b) -> p a b", a=shape[1])
        elif len(shape) == 4:
            a = a.rearrange("p (a b c) -> p a b c", a=shape[1], b=shape[2])
        assert tuple(a.shape) == tuple(shape), (a.shape, shape)
        return a

    areset()
    wmb = carve([128, 8, 3 * D], BF16)
    areset()
    ckvn = carve([128, 512], BF16)
    krsq = carve([32, 512], BF16)
    Kst = carve([96, 8, 512], BF16)
    Vst = carve([128, 4, 8, 128], BF16)
    sqk = carve([128, 4, 64])
    ssqn = carve([128, 8])
    ssqr = carve([128, 8])
    xsq_p1 = carve([128, D])
    areset()
    TW = carve([128, 32, 384], BF16)
    ZT = carve([128, 2, 2, NX], BF16)
    ABs = [carve([128, 512], BF16) for i in range(2)]
    Vfs = [carve([128, 512], BF16) for i in range(2)]
    FTc = carve([128, 2, 256], BF16)
    areset()
    cqn = carve([128, 2, 512], BF16)
    sag = carve([128, 4, 512], BF16)
    ug = carve([128, 2, 512])
    QT = carve([96, 8, 512], BF16)
    NKH = 9
    Kb = [carve([96, NKH * 128], BF16) for i in range(2)]
    Vb = [carve([128, NKH, 128], BF16) for i in range(2)]
    PT = [carve([128, 512], BF16) for i in range(3)]
    mixT = carve([128, 8, 512], BF16)
    ost = [carve([128, D]) for i in range(2)]
    rbs = carve([64, 512])
    otmp = carve([128, 512])
    vnp = carve([128, 4, 128], BF16)
    vtmp = carve([128, 256])
    gst = carve([128, 16])
    gt = carve([128, 128])
    xsq_p3 = carve([128, D])
    tc1 = carve([32, 512])
    print("sbuf bytes remaining", nc.sbuf_bytes_remaining)

    psA = nc.alloc_psum_tensor("psA", [128, 512], F32)
    psB = nc.alloc_psum_tensor("psB", [128, 512], F32)
    psT = nc.alloc_psum_tensor("psT", [128, 1024], BF16)
    psS = [nc.alloc_psum_tensor("psS%d" % i, [128, 512], F32) for i in range(2)]
    psO = [nc.alloc_psum_tensor("psO%d" % i, [128, 512], F32) for i in range(2)]
    psM = nc.alloc_psum_tensor("psM", [128, 512], F32)
    P.excl = {"psA", "psB", "psT", "psS0", "psS1", "psO0", "psO1", "psM"}

    def mm(out_, lhsT, rhs, start, stop, r, w):
        P.add("pe", lambda e: e.matmul(out_, lhsT, rhs, start=start, stop=stop), reads=r, writes=w)

    def tr(out_, in_, r, w):
        P.add("pe", lambda e: e.transpose(out_, in_, ident[:]), reads=list(r) + ["ident"], writes=w)

    def act(out_, in_, func, r, w, scale=1.0):
        P.add("act", lambda e: e.activation(out=out_, in_=in_, func=func, scale=scale), reads=r, writes=w)

    def ts(eng, out_, in0, s1, s2, op0, op1, r, w):
        if s2 is None:
            P.add(eng, lambda e: e.tensor_scalar(out=out_, in0=in0, scalar1=s1, scalar2=None, op0=op0), reads=r, writes=w)
        else:
            P.add(eng, lambda e: e.tensor_scalar(out=out_, in0=in0, scalar1=s1, scalar2=s2, op0=op0, op1=op1), reads=r, writes=w)

    def tt(eng, out_, in0, in1, op, r, w):
        P.add(eng, lambda e: e.tensor_tensor(out=out_, in0=in0, in1=in1, op=op), reads=r, writes=w)

    def stt(eng, out_, in0, scalar, in1, op0, op1, r, w):
        P.add(eng, lambda e: e.scalar_tensor_tensor(out=out_, in0=in0, scalar=scalar, in1=in1, op0=op0, op1=op1), reads=r, writes=w)

    def cp(eng, out_, in_, r, w):
        if eng == "act":
            P.add("act", lambda e: e.copy(out=out_, in_=in_), reads=r, writes=w)
        else:
            P.add(eng, lambda e: e.tensor_copy(out=out_, in_=in_), reads=r, writes=w)

    def red(out_, in_, r, w):
        P.add("dve", lambda e: e.tensor_reduce(out=out_, in_=in_, axis=AX.X, op=ALU.add), reads=r, writes=w)

    def recip(out_, in_, r, w):
        P.add("dve", lambda e: e.reciprocal(out=out_, in_=in_), reads=r, writes=w)

    def memset(eng, ap, val, w):
        P.add(eng, lambda e: e.memset(ap, val), writes=w)

    def dma(q, out_, in_, r, w, slow=False):
        if slow:
            P.add(q, lambda e: e.dma_start(out=out_, in_=in_, allow_slow_non_contiguous=True), reads=r, writes=w, dma=True)
        else:
            P.add(q, lambda e: e.dma_start(out=out_, in_=in_), reads=r, writes=w, dma=True)

    def rsqrt_ps(out_, psin, n, r, w):
        P.add("act", lambda e: e.activation(out=out_, in_=psin, func=AF.Ln, scale=1.0 / n, bias=EPS), reads=r, writes=w)
        act(out_, out_, AF.Exp, w, w, scale=-0.5)

    def rsqrt_to(out_, in_, n, r, w, width_key):
        act(out_, in_, AF.Ln, r, w, scale=1.0 / n)
        act(out_, out_, AF.Exp, w, w, scale=-0.5)

    memset("pool", ident[:], 0.0, ["ident"])
    P.add("pool", lambda e: e.affine_select(out=ident[:], in_=ident[:], compare_op=ALU.not_equal, fill=1.0, base=0,
                                            pattern=[[-1, 128]], channel_multiplier=1), reads=["ident"], writes=["ident"])
    memset("pool", ones_bf[:], 1.0, ["ones_bf"])
    memset("pool", onesq[:], 1.0, ["onesq"])
    memset("pool", onesq[32:64, :], 0.0, ["onesq"])
    dma("pool", tab[:], tab_d, [], ["tab"])
    dma("pool", csbd[:], csbd_d, [], ["csbd"])
    dma("pool", c64[:], c64_d, [], ["c64"])
    dma("pool", c256[:], c256_d.rearrange("p (t k) -> p t k", t=2), [], ["c256"])
    for jj in range(2):
        dma("sp", cv[:, :, jj], cvec[jj].rearrange("(k p) -> p k", p=128), [], ["cv"], slow=True)
    act(sc[:], cv[:], AF.Exp, ["cv"], ["sc"], scale=-1.0)
    ts("dve", sc[:], sc[:], 1.0, None, ALU.add, None, ["sc"], ["sc"])
    recip(sc[:], sc[:], ["sc"], ["sc"])
    tt("dve", sc[:], sc[:], cv[:], ALU.mult, ["sc", "cv"], ["sc"])
    cp("dve", scb[:], sc[:], ["sc"], ["scb"])

    bufs = [bufA, bufB]

    def load_x(src, t0, w, tis=None):
        nt = w // 128
        for ti in (range(nt) if tis is None else tis):
            if ti < nt:
                dma("sp", xt[:, ti, :], src[t0 + ti * 128:t0 + (ti + 1) * 128, :], [("xy", id(src), (t0 // 128) + ti)], [("xt", ti)])

    def build_hT(src, t0, w, j, xsq, loaded=False):
        nt = w // 128
        if not loaded:
            load_x(src, t0, w)
        for ti in range(nt):
            act(xsq[:], xt[:, ti, :], AF.Square, [("xt", ti)], ["xsq"])
            red(ssq[:, ti:ti + 1], xsq[:], ["xsq"], ["ssq"])
        ts("dve", ssq[:, 0:nt], ssq[:, 0:nt], D * EPS, None, ALU.add, None, ["ssq"], ["ssq"])
        rsqrt_to(rstd[:, 0:nt], ssq[:, 0:nt], D, ["ssq"], ["rstd"], None)
        for ti in range(nt):
            xsb = xs[ti % 2]
            ts("dve", xsb[:], xt[:, ti, :], rstd[:, ti:ti + 1], None, ALU.mult, None, [("xt", ti), "rstd"], [("xs", ti % 2)])
            for k in range(8):
                tr(psT[:, k * 128:(k + 1) * 128], xsb[:, k * 128:(k + 1) * 128], [("xs", ti % 2)], ["psT"])
            for k in range(8):
                ts("dve", hT[:, k, ti * 128:(ti + 1) * 128], psT[:, k * 128:(k + 1) * 128],
                   amod[:, k, j:j + 1], modv[:, k, j:j + 1], ALU.mult, ALU.add, ["psT", "amod", "modv"], ["hT"])

    def key(ps):
        return ps.name

    def proj_fm(ps, col0, ncol, w, wkey="w_in"):
        for k in range(8):
            mm(ps[0:ncol, 0:w], w_in_sb[:, k, col0:col0 + ncol], hT[:, k, 0:w], k == 0, k == 7, ["hT", (wkey, k)], [key(ps)])

    blocks = [(0, 256, 1)] + [(NC_ + 512 * i, 512, 0) for i in range(8)]

    for l in range(depth):
        src = xin if l == 0 else bufs[(l - 1) % 2]
        last = (l == depth - 1)
        dst = out if last else bufs[l % 2]
        if phases < 1:
            break
        P.barrier()
        for k in range(8):
            dma("pool", w_in_sb[:, k, :], w_in[l, k * 128:(k + 1) * 128, :], [], [("w_in", k)])
        for k in range(8):
            dma("pool", w_out_sb[:, k, :], w_out[l, k * 128:(k + 1) * 128, :], [], [("w_out", k)])
        for k in range(2):
            dma("pool", wq_sb[:, k, :], w_uq[l, k * 128:(k + 1) * 128, :], [], ["wq"])
            dma("pool", wfm_sb[:, k, :], w_fmix[l, k * 128:(k + 1) * 128, :], [], ["wfm"])
        dma("pool", wukv_sb[:], w_ukv[l], [], ["wukv"])
        for h in range(4):
            dma("pool", wsT_sb[:, h, :], wsT_d[l, h], [], ["wsT"])
        dma("sp", vecs[:], vecs_d[l], [], ["vecs"])
        dma("sp", glnbc[:], glnbc_d[l], [], ["glnbc"])
        dma("sp", bsbc[:], bsbc_d[l].rearrange("p (a q) -> p a q", a=2), [], ["bsbc"])
        for k in range(8):
            dma("pool", wmb[:, k, :], w_mod[l, k * 128:(k + 1) * 128, :], [], [("wmb", k)])
        for f in range(24):
            for k in range(8):
                mm(psM[:, 2 * f:2 * f + 2], wmb[:, k, f * 128:(f + 1) * 128], scb[:, k, :], k == 0, k == 7, [("wmb", k), "scb"], ["psM"])
        pm = psM[:, 0:48].rearrange("p (f j) -> p f j", j=2)
        for jj in range(2):
            tt("dve", modv[:, :, jj], pm[:, :, jj], vecs[:, V_BM:V_BM + 24], ALU.add, ["psM", "vecs"], ["modv"])
        for jj in range(2):
            stt("dve", amod[:, :, jj], modv[:, 8:16, jj], 1.0, vecs[:, V_NG:V_NG + 8], ALU.add, ALU.mult, ["modv", "vecs"], ["amod"])
        for jj in range(2):
            dma("sp", gscr[jj].rearrange("(k p) -> p k", p=128), modv[:, 16:24, jj], ["modv"], ["gscr"], slow=True)
        for jj in range(2):
            dma("sp", gate_bc[:, jj, :], gscr[jj].partition_broadcast(128), ["gscr"], ["gate_bc"])

        if phases < 2:
            break
        P.barrier()
        memset("pool", Vst[:, :, :, 64:128], 1.0, ["Vst"])
        for (t0, w, j) in blocks:
            nt = w // 128
            tt0 = t0 // 128
            bidx = blocks.index((t0, w, j))
            build_hT(src, t0, w, j, xsq_p1, loaded=(bidx > 0))
            if bidx + 1 < len(blocks):
                load_x(src, blocks[bidx + 1][0], blocks[bidx + 1][1])
            if dbg and "hTp1" in dbg_out and l == DBG_L and t0 == NC_:
                dma("pool", dbg_out["hTp1"].rearrange("p (a b) -> p a b", a=8), hT[:, :, :], ["hT"], ["dbghTp1"])
                dma("pool", dbg_out["win"], w_in_sb[:, :, 0:256], ["w_in"], ["dbgwin"])
            if sub < 2:
                break
            for cc in range(2):
                ps = psA if cc == 0 else psB
                proj_fm(ps, cc * 128, 128, w)
                cp("act" if cc == 0 else "dve", UT[:, cc, t0:t0 + w], ps[:, 0:w], [key(ps)], ["UT"])
            if sub < 3:
                break
            proj_fm(psA, 1536, 128, w)
            act(sqb[0][:, 0:w], psA[:, 0:w], AF.Square, ["psA"], [("sqb", 0)])
            mm(psM[:, 0:w], ones_bf[:], sqb[0][:, 0:w], True, True, [("sqb", 0), "ones_bf"], ["psM"])
            ts("dve", rbc[:, 0:w], psM[:, 0:w], 128 * EPS, None, ALU.add, None, ["psM"], ["rbc"])
            rsqrt_to(rbc[:, 0:w], rbc[:, 0:w], 128, ["rbc"], ["rbc"], None)
            stt("dve", ckvn[:, 0:w], psA[:, 0:w], vecs[:, V_KVAG:V_KVAG + 1], rbc[:, 0:w], ALU.mult, ALU.mult, ["psA", "vecs", "rbc"], ["ckvn"])
            if sub < 4:
                break
            proj_fm(psB, 1664, 64, w)
            act(krsq[:, 0:w], psB[0:32, 0:w], AF.Square, ["psB"], ["krsq"])
            stt("dve", tmpA[0:64, 0:w], psB[0:64, 0:w], vecs[0:64, V_GKE:V_GKE + 1], tab[:, t0:t0 + w], ALU.mult, ALU.mult,
                ["psB", "vecs", "tab"], ["tmpA"])
            cp("dve", tmpB[0:32, 0:w], tmpA[32:64, 0:w], ["tmpA"], ["tmpB"])
            tt("dve", tmpA[64:96, 0:w], tmpA[0:32, 0:w], tmpB[0:32, 0:w], ALU.add, ["tmpA", "tmpB"], ["tmpA"])
            if sub < 5:
                break
            for h in range(8):
                hb = h % 2
                ps = psA if hb == 0 else psB
                rq = rbc if hb == 0 else rbq1
                rqk = "rbc" if hb == 0 else "rbq1"
                mm(ps[0:64, 0:w], wukv_sb[:, h * 128:h * 128 + 64], ckvn[:, 0:w], True, True, ["wukv", "ckvn"], [key(ps)])
                act(sqb[hb][0:64, 0:w], ps[0:64, 0:w], AF.Square, [key(ps)], [("sqb", hb)])
                mm(psM[0:96, 0:w], ones_bf[0:64, 0:96], sqb[hb][0:64, 0:w], True, False, [("sqb", hb), "ones_bf"], ["psM"])
                mm(psM[0:96, 0:w], ones_bf[0:32, 0:96], krsq[0:32, 0:w], False, True, ["krsq", "ones_bf"], ["psM"])
                P.add("act", (lambda o_, i_: (lambda e: e.activation(out=o_, in_=i_, func=AF.Ln, scale=1.0 / 96, bias=EPS)))(rq[0:96, 0:w], psM[0:96, 0:w]),
                      reads=["psM"], writes=[rqk])
                P.add("act", (lambda o_: (lambda e: e.activation(out=o_, in_=o_, func=AF.Exp, scale=-0.5, bias=KSC)))(rq[0:96, 0:w]),
                      reads=[rqk], writes=[rqk])
                stt("dve", Kst[0:64, h, 0:w], ps[0:64, 0:w], vecs[0:64, V_GKN:V_GKN + 1], rq[0:64, 0:w], ALU.mult, ALU.mult,
                    [key(ps), "vecs", rqk], [("Kst", "n")])
                tt("dve", Kst[64:96, h, 0:w], tmpA[64:96, 0:w], rq[64:96, 0:w], ALU.mult, ["tmpA", rqk], [("Kst", "r")])
            if sub < 6:
                break
            for h in range(8):
                dma("sp", Kscr[h, :, t0:t0 + w], Kst[:, h, 0:w], [("Kst", "n"), ("Kst", "r")], ["Kscr"])
            if sub < 7:
                break
            import os
            CUT = int(os.environ.get("CUT", "99"))
            for ti in range(nt):
                for half in range(2):
                    ps = psA if half == 0 else psB
                    if CUT >= 1:
                        mm(ps[:, 0:512], ckvn[:, ti * 128:(ti + 1) * 128], wukv_sb[:, half * 512:(half + 1) * 512], True, True, ["ckvn", "wukv"], [key(ps)])
                    pv = ps[:, 0:512].rearrange("p (h c) -> p h c", c=128)
                    if CUT >= 2:
                        cp("dve", Vst[:, ti, half * 4:half * 4 + 4, 0:64], pv[:, :, 64:128], [key(ps)], ["Vst"])
            if sub < 8:
                break
            for h in range(8):
                dma("sp", Vscr[h].rearrange("p (t c) -> p t c", c=128)[:, tt0:tt0 + nt, :], Vst[:, 0:nt, h, :], ["Vst"], ["Vscr"])

        if phases < 3:
            break
        P.barrier()
        dma("pool", TW[:], tw_d.rearrange("p (j c) -> p j c", c=384), [], ["TW"])
        if dbg and "UT" in dbg_out and l == DBG_L:
            dma("pool", dbg_out["UT"].rearrange("p (a b) -> p a b", a=2), UT[:, :, :], ["UT"], ["dbgUT"])
            dma("sp", dbg_out["sk"], sk[:].rearrange("p a b -> p (a b)"), ["sk"], ["dbgsk"])
        for t in range(2):
            for cc in range(2):
                mm(psA[:, cc * 256:(cc + 1) * 256], UT[:, cc, t * 128:(t + 1) * 128], csbd[:], True, True, ["UT", "csbd"], ["psA"])
            cp("dve", ABs[t][:], psA[:], ["psA"], [("ABs", t)])
        for cc in range(2):
            for t in range(2):
                mm(psB[:, cc * 256:(cc + 1) * 256], ABs[t][:, cc * 256:cc * 256 + 128], c256[:, t, 0:256], t == 0, False, [("ABs", t), "c256"], ["psB"])
                mm(psB[:, cc * 256:(cc + 1) * 256], ABs[t][:, cc * 256 + 128:cc * 256 + 256], c256[:, t, 256:512], False, t == 1, [("ABs", t), "c256"], ["psB"])
        cp("dve", FTc[:], psB[:].rearrange("p (a k) -> p a k", a=2), ["psB"], ["FTc"])
        for c2 in range(2):
            for cc in range(2):
                mm(psA[:, c2 * 256:(c2 + 1) * 256], wfm_sb[:, cc, c2 * 128:(c2 + 1) * 128], FTc[:, cc, :], cc == 0, cc == 1, ["wfm", "FTc"], ["psA"])
        cp("dve", UT[:, :, 0:256], psA[:].rearrange("p (a k) -> p a k", a=2), ["psA"], ["UT"])
        for jq in range(32):
            p1, p2 = (psA, psB) if jq % 2 == 0 else (psS[0], psS[1])
            ab = ABs[jq % 2]
            for cc in range(2):
                for clo in range(2):
                    c = 2 * jq + clo
                    mm(p1[clo * 64:(clo + 1) * 64, cc * 256:(cc + 1) * 256], UT[:, cc, NC_ + c:NC_ + NX:64], csbd[:], True, True, ["UT", "csbd"], [key(p1)])
            cp("act" if jq % 2 == 0 else "dve", ab[:], p1[:], [key(p1)], [("ABs", jq % 2)])
            for cc in range(2):
                mm(p2[:, cc * 256:(cc + 1) * 256], ab[:, cc * 256:cc * 256 + 128], TW[:, jq, 128:384], True, False, [("ABs", jq % 2), "TW"], [key(p2)])
                mm(p2[:, cc * 256:(cc + 1) * 256], ab[:, cc * 256 + 128:cc * 256 + 256], TW[:, jq, 0:256], False, True, [("ABs", jq % 2), "TW"], [key(p2)])
            cp("dve" if jq % 2 == 0 else "act", ZT[:, :, :, jq * 128:(jq + 1) * 128], p2[:].rearrange("p (a b k) -> p a b k", a=2, b=2), [key(p2)], ["ZT"])
        for iq in range(32):
            p1, p2 = (psA, psB) if iq % 2 == 0 else (psS[0], psS[1])
            vf = Vfs[iq % 2]
            for plo in range(2):
                pp = 2 * iq + plo
                for ri in range(2):
                    for cc in range(2):
                        mm(p1[plo * 64:(plo + 1) * 64, ri * 256:(ri + 1) * 256], ZT[:, cc, ri, pp:NX:64], wfm_sb[:, cc, :], cc == 0, cc == 1, ["ZT", "wfm"], [key(p1)])
            cp("act" if iq % 2 == 0 else "dve", vf[:], p1[:], [key(p1)], [("Vfs", iq % 2)])
            for c2 in range(2):
                mm(p2[:, c2 * 128:(c2 + 1) * 128], vf[:, c2 * 128:(c2 + 1) * 128], c64[:, 0:128], True, False, [("Vfs", iq % 2), "c64"], [key(p2)])
                mm(p2[:, c2 * 128:(c2 + 1) * 128], vf[:, 256 + c2 * 128:256 + (c2 + 1) * 128], c64[:, 128:256], False, True, [("Vfs", iq % 2), "c64"], [key(p2)])
            for c2 in range(2):
                ov = UT[:, c2, NC_:T].rearrange("p (q r) -> p r q", r=64)[:, 2 * iq:2 * iq + 2, :]
                cp("dve" if iq % 2 == 0 else "act", ov, p2[:, c2 * 128:(c2 + 1) * 128].rearrange("p (a q) -> p a q", a=2), [key(p2)], ["UT"])

        if phases < 4:
            break
        P.barrier()
        memset("pool", vnp[:], 0.0, ["vnp"])
        import os
        CUT3 = int(os.environ.get('CUT3', '99'))
        p3blocks = blocks[1:] if last else blocks
        p3blocks = p3blocks[:int(os.environ.get('NB3', '99'))]
        chunks = []
        for bi, (t0, w, j) in enumerate(p3blocks):
            for h in range(8):
                if j == 1:
                    chunks.append((bi, h, 0, 2))
                else:
                    for k0_ in range(0, NT, NKH):
                        chunks.append((bi, h, k0_, min(NT, k0_ + NKH)))
        nchunk = len(chunks)

        def issue_chunk(ci):
            bi_, h_, k0, k1 = chunks[ci]
            s = ci % 2
            nk = k1 - k0
            dma("sp", Kb[s][:, 0:nk * 128], Kscr[h_, :, k0 * 128:k1 * 128], ["Kscr"], [("Kb", s)])
            dma("sp", Vb[s][:, 0:nk, :], Vscr[h_].rearrange("p (t c) -> p t c", c=128)[:, k0:k1, :], ["Vscr"], [("Vb", s)])

        issue_chunk(0)
        if nchunk > 1:
            issue_chunk(1)
        ci = 0
        for bi, (t0, w, j) in enumerate(p3blocks):
            nt = w // 128
            tt0 = t0 // 128
            build_hT(src, t0, w, j, xsq_p3, loaded=(bi > 0))

            def silu_r(ps, r_out, rk):
                act(r_out, ps[:, 0:w], AF.Exp, [key(ps)], [rk], scale=-1.0)
                P.add("act", lambda e: e.activation(out=r_out, in_=r_out, func=AF.Ln, bias=1.0), reads=[rk], writes=[rk])
                act(r_out, r_out, AF.Exp, [rk], [rk], scale=-1.0)

            if CUT3 < 2:
                break
            for cc in range(2):
                ps = psA if cc == 0 else psB
                proj_fm(ps, 256 + cc * 128, 128, w)
                r_ = tmpA[:, 0:w] if cc == 0 else tmpB[:, 0:w]
                rk = "tmpA" if cc == 0 else "tmpB"
                silu_r(ps, r_, rk)
                tt("dve", r_, ps[:, 0:w], r_, ALU.mult, [key(ps), rk], [rk])
                tt("dve", mixT[:, cc, 0:w], r_, UT[:, cc, t0:t0 + w], ALU.mult, [rk, "UT"], [("mixT", cc)])
            if CUT3 < 3:
                break
            for cc in range(2):
                proj_fm(psA, 512 + cc * 128, 128, w)
                proj_fm(psB, 1024 + cc * 128, 128, w)
                r_ = tmpC[:, 0:w]
                silu_r(psB, r_, "tmpC")
                tt("dve", r_, psB[:, 0:w], r_, ALU.mult, ["psB", "tmpC"], ["tmpC"])
                tt("dve", ug[:, cc, 0:w], psA[:, 0:w], r_, ALU.mult, ["psA", "tmpC"], ["ug"])
            if CUT3 < 4:
                break
            proj_fm(psA, 1280, 128, w)
            proj_fm(psB, 1408, 128, w)
            act(sqb[0][:, 0:w], psA[:, 0:w], AF.Square, ["psA"], [("sqb", 0)])
            act(sqb[1][:, 0:w], psB[:, 0:w], AF.Square, ["psB"], [("sqb", 1)])
            mm(psM[:, 0:w], ones_bf[:], sqb[0][:, 0:w], True, False, [("sqb", 0), "ones_bf"], ["psM"])
            mm(psM[:, 0:w], ones_bf[:], sqb[1][:, 0:w], False, True, [("sqb", 1), "ones_bf"], ["psM"])
            ts("dve", rbc[:, 0:w], psM[:, 0:w], 256 * EPS, None, ALU.add, None, ["psM"], ["rbc"])
            rsqrt_to(rbc[:, 0:w], rbc[:, 0:w], 256, ["rbc"], ["rbc"], None)
            stt("dve", cqn[:, 0, 0:w], psA[:, 0:w], vecs[:, V_QAG:V_QAG + 1], rbc[:, 0:w], ALU.mult, ALU.mult, ["psA", "vecs", "rbc"], ["cqn"])
            stt("dve", cqn[:, 1, 0:w], psB[:, 0:w], vecs[:, V_QAG + 1:V_QAG + 2], rbc[:, 0:w], ALU.mult, ALU.mult, ["psB", "vecs", "rbc"], ["cqn"])
            if CUT3 < 5:
                break
            for ac in range(4):
                ps = psA if ac % 2 == 0 else psB
                proj_fm(ps, 1728 + ac * 128, 128, w)
                r_ = tmpA[:, 0:w] if ac % 2 == 0 else tmpB[:, 0:w]
                rk = "tmpA" if ac % 2 == 0 else "tmpB"
                silu_r(ps, r_, rk)
                tt("dve", sag[:, ac, 0:w], ps[:, 0:w], r_, ALU.mult, [key(ps), rk], ["sag"])
            if CUT3 < 6:
                break
            for ti in range(nt):
                tsl = slice(ti * 128, (ti + 1) * 128)
                for k in range(8):
                    mm(psA[:, 0:256], hT[:, k, tsl], w_in_sb[:, k, 768:1024], k == 0, k == 7, ["hT", ("w_in", k)], ["psA"])
                pv = psA[:, 0:256].rearrange("p (h c) -> p h c", c=64)
                red(gst[:, 0:4], pv, ["psA"], ["gst"])
                act(vtmp[:], psA[:, 0:256], AF.Square, ["psA"], ["vtmp"])
                red(gst[:, 4:8], vtmp[:].rearrange("p (h c) -> p h c", c=64), ["vtmp"], ["gst"])
                ts("dve", gst[:, 0:4], gst[:, 0:4], 1.0 / 64, None, ALU.mult, None, ["gst"], ["gst"])
                tt("dve", gst[:, 8:12], gst[:, 0:4], gst[:, 0:4], ALU.mult, ["gst"], ["gst"])
                stt("dve", gst[:, 4:8], gst[:, 4:8], 1.0 / 64, gst[:, 8:12], ALU.mult, ALU.subtract, ["gst"], ["gst"])
                ts("dve", gst[:, 4:8], gst[:, 4:8], EPS, None, ALU.add, None, ["gst"], ["gst"])
                rsqrt_to(gst[:, 12:16], gst[:, 4:8], 1.0, ["gst"], ["gst"], None)
                for h in range(4):
                    ts("dve", vtmp[:, h * 64:(h + 1) * 64], psA[:, h * 64:(h + 1) * 64], gst[:, h:h + 1], gst[:, 12 + h:13 + h],
                       ALU.subtract, ALU.mult, ["psA", "gst"], ["vtmp"])
                vo = vnp[:].rearrange("p (a b) c -> p a b c", b=2)
                for par in range(2):
                    tt("pool", vo[:, :, par, par * 64:par * 64 + 64],
                       vtmp[:].rearrange("p (a b c) -> p a b c", b=2, c=64)[:, :, par, :],
                       glnbc[:].rearrange("p (a b c) -> p a b c", b=2, c=64)[:, :, par, :], ALU.mult, ["vtmp", "glnbc"], ["vnp"])
                for pr in range(2):
                    ps = psB
                    for par in range(2):
                        h = 2 * pr + par
                        mm(ps[:, pr * 128:(pr + 1) * 128], vnp[:, h, :], wsT_sb[:, h, :], par == 0, par == 1, ["vnp", "wsT"], ["psB"])
                for pr in range(2):
                    tt("dve", gt[:], psB[:, pr * 128:(pr + 1) * 128], bsbc[:, pr, :], ALU.add, ["psB", "bsbc"], ["gt"])
                    tt("pool", mixT[:, 2 + pr, tsl], gt[:], ug[:, pr, tsl], ALU.mult, ["gt", "ug"], [("mixT", 2 + pr)])
            if CUT3 < 7:
                break
            for h in range(8):
                hb = h % 2
                ps = psA if hb == 0 else psB
                sq_ = sqb[hb]
                tq = tmpD if hb == 0 else tq1
                rq = rbc if hb == 0 else rbq1
                tcb = tmpC if hb == 0 else tc1
                tqk, rqk, tck = ("tmpD", "rbc", "tmpC") if hb == 0 else ("tq1", "rbq1", "tc1")
                for kc in range(2):
                    mm(ps[:, 0:w], wq_sb[:, kc, h * 128:(h + 1) * 128], cqn[:, kc, 0:w], kc == 0, kc == 1, ["wq", "cqn"], [key(ps)])
                act(sq_[:, 0:w], ps[:, 0:w], AF.Square, [key(ps)], [("sqb", hb)])
                mm(psM[:, 0:w], onesq[:], sq_[:, 0:w], True, True, [("sqb", hb), "onesq"], ["psM"])
                rsqrt_ps(rq[:, 0:w], psM[:, 0:w], 96, ["psM"], [rqk])
                stt("dve", tq[:, 0:w], ps[:, 0:w], vecs[:, V_GQ:V_GQ + 1], rq[:, 0:w], ALU.mult, ALU.mult, [key(ps), "vecs", rqk], [tqk])
                cp("dve", QT[0:64, h, 0:w], tq[64:128, 0:w], [tqk], [("QT", h)])
                tt("dve", tq[0:64, 0:w], tq[0:64, 0:w], tab[:, t0:t0 + w], ALU.mult, [tqk, "tab"], [tqk])
                cp("dve", tcb[0:32, 0:w], tq[32:64, 0:w], [tqk], [tck])
                tt("dve", QT[64:96, h, 0:w], tq[0:32, 0:w], tcb[0:32, 0:w], ALU.add, [tqk, tck], [("QT", h)])
            if CUT3 < 8:
                break

            def epilogue(h_, po):
                cp("dve", rbs[:, 0:w], tmpD[64:128, 0:w], ["tmpD"], ["rbs"])
                pa = (h_ % 2) * 64
                tt("dve", otmp[pa:pa + 64, 0:w], po[0:64, 0:w], rbs[:, 0:w], ALU.mult, [key(po), "rbs"], ["otmp"])
                tt("dve", mixT[pa:pa + 64, 4 + h_ // 2, 0:w], otmp[pa:pa + 64, 0:w], sag[pa:pa + 64, h_ // 2, 0:w], ALU.mult,
                   ["otmp", "sag"], [("mixT", 4 + h_ // 2, h_ % 2)])

            tiles = []
            cj = ci
            for h in range(8):
                nsub = 1 if j == 1 else (NT + NKH - 1) // NKH
                tot_kt = 2 if j == 1 else NT
                n_ = 0
                for sbi in range(nsub):
                    _, _, k0, k1 = chunks[cj]
                    for kt in range(k0, k1):
                        tiles.append((h, kt, cj, kt - k0, n_ == 0, n_ == tot_kt - 1, kt == k1 - 1))
                        n_ += 1
                    cj += 1
            ci = cj

            def emit_qk(i):
                h_, kt_, c_, kl_, _, _, _ = tiles[i]
                pss = psS[i % 2]
                mm(pss[:, 0:w], Kb[c_ % 2][:, kl_ * 128:(kl_ + 1) * 128], QT[:, h_, 0:w], True, True, [("Kb", c_ % 2), ("QT", h_)], [key(pss)])

            emit_qk(0)
            for i in range(len(tiles)):
                h_, kt_, c_, kl_, first_, last_, chunk_end = tiles[i]
                if i + 1 < len(tiles):
                    emit_qk(i + 1)
                pss = psS[i % 2]
                pt = PT[i % 3]
                po = psO[h_ % 2]
                act(pt[:, 0:w], pss[:, 0:w], AF.Exp, [key(pss)], [("PT", i % 3)])
                mm(po[:, 0:w], Vb[c_ % 2][:, kl_, :], pt[:, 0:w], first_, last_, [("Vb", c_ % 2), ("PT", i % 3)], [key(po)])
                if chunk_end and c_ + 2 < nchunk:
                    issue_chunk(c_ + 2)
                if last_:
                    recip(tmpD[64:128, 0:w], po[64:128, 0:w], [key(po)], ["tmpD"])
                    epilogue(h_, po)
            if dbg and "hT" in dbg_out and l == DBG_L and bi == 0:
                dma("pool", dbg_out["hT"].rearrange("p (a b) -> p a b", a=8), hT[:, :, :], ["hT"], ["dbghT"])
                dma("sp", dbg_out["modv"], modv[:].rearrange("p a b -> p (a b)"), ["modv"], ["dbgmodv"])
                dma("sp", dbg_out["amod"], amod[:].rearrange("p a b -> p (a b)"), ["amod"], ["dbgamod"])
            if dbg and "mix" in dbg_out and l == DBG_L and bi == 0:
                _mk = [("mixT", 0), ("mixT", 1), ("mixT", 2), ("mixT", 3)] + [("mixT", 4 + a, b) for a in range(4) for b in range(2)]
                dma("pool", dbg_out["mix"].rearrange("p (a b) -> p a b", a=8), mixT[:, :, :], _mk, ["dbgmix"])
            mixkeys = [("mixT", 0), ("mixT", 1), ("mixT", 2), ("mixT", 3)] + [("mixT", 4 + a, b) for a in range(4) for b in range(2)]
            for ti in range(nt):
                tsl = slice(ti * 128, (ti + 1) * 128)
                o_ = ost[ti % 2]
                for half in range(2):
                    ps = psA if half == 0 else psB
                    for kc in range(8):
                        mm(ps[:, 0:512], mixT[:, kc, tsl], w_out_sb[:, kc, half * 512:(half + 1) * 512], kc == 0, kc == 7, mixkeys + [("w_out", kc)], [key(ps)])
                    tmp_ = tmpA if half == 0 else tmpB
                    tk = "tmpA" if half == 0 else "tmpB"
                    tt("dve", tmp_[:], ps[:, 0:512], gate_bc[:, j, half * 512:(half + 1) * 512], ALU.mult, [key(ps), "gate_bc"], [tk])
                    tt("pool", o_[:, half * 512:(half + 1) * 512], tmp_[:], xt[:, ti, half * 512:(half + 1) * 512], ALU.add, [tk, ("xt", ti)], [("ost", ti % 2)])
                if bi + 1 < len(p3blocks):
                    load_x(src, p3blocks[bi + 1][0], p3blocks[bi + 1][1], tis=[ti] if ti < nt - 1 else list(range(ti, 4)))
                r0 = t0 + ti * 128
                if last:
                    dma("sp", dst[r0 - NC_:r0 - NC_ + 128, :], o_[:], [("ost", ti % 2)], [("xy", id(dst), tt0 + ti)])
                else:
                    dma("sp", dst[r0:r0 + 128, :], o_[:], [("ost", ti % 2)], [("xy", id(dst), tt0 + ti)])
    if dbg and "bufA" in dbg_out:
        P.barrier()
        dma("sp", dbg_out["bufA"], bufA, [], ["dbgbufA"])
    P.emit()
    return nc


_NC_CACHE = {}


def kernel(**inputs):
    inp = {k: np.asarray(v) for k, v in inputs.items()}
    if "nc" not in _NC_CACHE:
        _NC_CACHE["nc"] = build(DEPTH)
    nc = _NC_CACHE["nc"]
    tab, csbd, c64s64, tw, c256 = _consts()
    lw = _layout_weights(inp)
    shared = dict(
        w_mod=np.ascontiguousarray(inp["w_mod"], np.float32), w_in_ext=lw["w_in_ext"],
        w_fmix=np.ascontiguousarray(inp["w_fmix"], np.float32), wsT=lw["wsT"], w_uq_ext=lw["w_uq_ext"],
        w_ukv=np.ascontiguousarray(inp["w_ukv"], np.float32), w_out=np.ascontiguousarray(inp["w_out"], np.float32),
        vecs=lw["vecs"], glnbc=lw["glnbc"], bsbc=lw["bsbc"], tab=tab, csbd=csbd, c64s64=c64s64, tw=tw, c256=c256)
    in_maps = []
    for b in range(8):
        m = dict(shared)
        m["xin"] = np.ascontiguousarray(np.concatenate([inp["ctx"][b], inp["x"][b]], axis=0), np.float32)
        m["cvec"] = np.ascontiguousarray(np.stack([inp["c"][b], inp["c_ctx"]], axis=0), np.float32)
        in_maps.append(m)
    res = run_bass_kernel_spmd(nc, in_maps, core_ids=list(range(8)))
    return np.stack([np.asarray(r["out"], np.float32) for r in res.results], axis=0)
```

```python
import contextlib
import numpy as np
import concourse.bass as bass
import concourse.mybir as mybir
from concourse.bass_utils import run_bass_kernel_spmd

F32 = mybir.dt.float32
BF16 = mybir.dt.bfloat16
AF = mybir.ActivationFunctionType
ALU = mybir.AluOpType
AX = mybir.AxisListType

DEPTH = 4
D = 1024
NX = 4096
NC_ = 256
T = NX + NC_
NT = T // 128
EPS = 1e-6
ENGS = ("pe", "act", "dve", "pool", "sp")


class _Instr:
    __slots__ = ("eng", "fn", "deps", "dma", "idx", "sig", "dsem", "dval")

    def __init__(self, eng, fn, deps, dma, idx):
        self.eng, self.fn, self.deps, self.dma, self.idx = eng, fn, deps, dma, idx
        self.sig = None
        self.dsem = None
        self.dval = None


class Prog:
    def __init__(self, nc, n_dma_sems=24):
        self.nc = nc
        self.instrs = []
        self.last_write = {}
        self.readers = {}
        self.n_dma_sems = n_dma_sems
        self.excl = set()
        self.since_bar_dma = []
        self.last_eng = {}

    def add(self, eng, fn, reads=(), writes=(), dma=False):
        idx = len(self.instrs)
        deps = set()
        instrs = self.instrs

        def same(o):
            p = instrs[o]
            return (not dma) and (not p.dma) and p.eng == eng and eng != "pool"

        xr = [b for b in reads if b in self.excl]
        if xr:
            reads = [b for b in reads if b not in self.excl]
            writes = list(writes) + [b for b in xr if b not in writes]
            for b in xr:
                w = self.last_write.get(b)
                if w is not None:
                    deps.add(w)
        for b in reads:
            w = self.last_write.get(b)
            if w is not None:
                deps.add(w)
        for b in writes:
            w = self.last_write.get(b)
            if w is not None and not same(w):
                deps.add(w)
            for r in self.readers.get(b, ()):
                if not same(r):
                    deps.add(r)
        ins = _Instr(eng, fn, deps, dma, idx)
        self.instrs.append(ins)
        for b in reads:
            self.readers.setdefault(b, []).append(idx)
        for b in writes:
            self.last_write[b] = idx
            self.readers[b] = []
        if dma:
            self.since_bar_dma.append(idx)
        else:
            self.last_eng[eng] = idx
        return idx

    def barrier(self):
        deps = set(self.last_eng.values()) | set(self.since_bar_dma)
        self.since_bar_dma = []
        for e in ENGS:
            idx = len(self.instrs)
            ins = _Instr(e, None, set(deps), False, idx)
            self.instrs.append(ins)

    def emit(self, final_wait_eng="sp"):
        nc = self.nc
        instrs = self.instrs
        needed = set()
        for ins in instrs:
            nd = set()
            for d in ins.deps:
                p = instrs[d]
                if p.eng == "pe" and ins.eng == "pe" and not p.dma and not ins.dma:
                    continue
                nd.add(d)
            ins.deps = nd
            needed.update(nd)
        LIM = 3000
        cnt = {e: 0 for e in ENGS}
        dcnt = [0] * self.n_dma_sems
        dlast = [None] * self.n_dma_sems
        NSP = 16
        nq = {"sp": 0, "pool": 0}
        for ins in instrs:
            if ins.dma:
                if ins.eng == "sp":
                    k = nq["sp"] % NSP
                else:
                    k = NSP + nq["pool"] % (self.n_dma_sems - NSP)
                nq[ins.eng] += 1
                prev = dlast[k]
                if prev is not None:
                    ins.deps.add(prev)
                dcnt[k] += 16
                ins.dsem, ins.dval = k, dcnt[k]
                dlast[k] = ins.idx
            elif ins.idx in needed and ins.fn is not None:
                n = cnt[ins.eng]
                cnt[ins.eng] += 1
                ins.sig = (n // LIM, n % LIM + 1)
        print('SEM COUNTS', cnt, 'dma', max(dcnt), 'ninstr', len(instrs), flush=True)
        with contextlib.ExitStack() as st:
            esem = {e: [st.enter_context(nc.semaphore("s_%s%d" % (e, i))) for i in range(cnt[e] // LIM + 1)] for e in ENGS}
            dsems = [st.enter_context(nc.semaphore("d%d" % i)) for i in range(self.n_dma_sems)]
            block = st.enter_context(nc.Block())
            per_eng = {e: [i for i in instrs if i.eng == e] for e in ENGS}
            final = {}
            for i in instrs:
                if i.dma:
                    final[i.dsem] = max(final.get(i.dsem, 0), i.dval)
            final_eng = {}
            for i in instrs:
                if i.sig is not None:
                    final_eng[i.eng] = i.sig
            nds = self.n_dma_sems

            def run(engname, eng):
                waited_e = {e: (0, 0) for e in ENGS}
                waited_d = [0] * nds
                for ins in per_eng[engname]:
                    need_e = {}
                    need_d = {}
                    for d in ins.deps:
                        p = instrs[d]
                        if p.dma:
                            if p.dval > waited_d[p.dsem]:
                                need_d[p.dsem] = max(need_d.get(p.dsem, 0), p.dval)
                        else:
                            if p.sig > waited_e[p.eng]:
                                need_e[p.eng] = max(need_e.get(p.eng, (0, 0)), p.sig)
                    for e, v in need_e.items():
                        eng.wait_ge(esem[e][v[0]], v[1])
                        waited_e[e] = v
                    for k, v in need_d.items():
                        eng.wait_ge(dsems[k], v)
                        waited_d[k] = v
                    if ins.fn is None:
                        continue
                    bi = ins.fn(eng)
                    if ins.dma:
                        bi.then_inc(dsems[ins.dsem], 16)
                    elif ins.sig is not None:
                        bi.then_inc(esem[ins.eng][ins.sig[0]], 1)
                if engname == final_wait_eng:
                    for k, v in final.items():
                        if v > waited_d[k]:
                            eng.wait_ge(dsems[k], v)
                    for e, v in final_eng.items():
                        if v > waited_e[e] and e != engname:
                            eng.wait_ge(esem[e][v[0]], v[1])

            @block.tensor
            def _(eng):
                run("pe", eng)

            @block.scalar
            def _(eng):
                run("act", eng)

            @block.vector
            def _(eng):
                run("dve", eng)

            @block.gpsimd
            def _(eng):
                run("pool", eng)

            @block.sync
            def _(eng):
                run("sp", eng)


_PERM = np.concatenate([np.arange(8, 16), np.arange(0, 8), np.arange(24, 32), np.arange(16, 24)])
_SIGN = np.concatenate([-np.ones(8), np.ones(8), -np.ones(8), np.ones(8)]).astype(np.float32)

V_NG, V_BM, V_QAG, V_KVAG, V_GQ, V_GKN, V_GKE = 0, 8, 32, 34, 35, 36, 37
NV = 38


def _consts():
    n = np.arange(NX)
    row = (n // 64).astype(np.float64)
    col = (n % 64).astype(np.float64)
    inv = 10000.0 ** (-np.arange(0, 16, 2, dtype=np.float64) / 16.0)
    ang = np.concatenate([row[:, None] * inv, row[:, None] * inv, col[:, None] * inv, col[:, None] * inv], -1)
    tab = np.zeros((64, T), np.float32)
    tab[0:32, :NC_] = 1.0
    tab[0:32, NC_:] = np.cos(ang).T
    tab[32:64, NC_:] = (np.sin(ang) * _SIGN[None, :]).T
    cc = np.arange(64)
    a64 = 2 * np.pi * np.outer(cc, cc) / 64.0
    C64, S64 = np.cos(a64), np.sin(a64)
    z = np.zeros((64, 64))
    bd = lambda m: np.block([[m, z], [z, m]])
    csbd = np.concatenate([bd(C64) / 8.0, -bd(S64) / 8.0], 1).astype(np.float32)
    c64s64 = np.concatenate([bd(C64) / 8.0, bd(S64) / 8.0], 1).astype(np.float32)
    tw = np.zeros((128, 32, 384), np.float32)
    r = np.arange(64)
    p = np.arange(64)
    for j in range(32):
        for clo in range(2):
            nn = 64 * r + 2 * j + clo
            a = 2 * np.pi * np.outer(nn, p) / NX
            sl = slice(clo * 64, clo * 64 + 64)
            tw[sl, j, 0 + clo * 64:0 + clo * 64 + 64] = np.sin(a) / 8.0
            tw[sl, j, 128 + clo * 64:128 + clo * 64 + 64] = np.cos(a) / 8.0
            tw[sl, j, 256 + clo * 64:256 + clo * 64 + 64] = -np.sin(a) / 8.0
    nn = np.arange(256)
    a256 = 2 * np.pi * np.outer(nn, nn) / 256.0
    c256 = np.concatenate([np.cos(a256) / 16.0, np.sin(a256) / 16.0], 1).astype(np.float32)
    c256 = c256.reshape(2, 128, 512).transpose(1, 0, 2).copy()
    return tab, csbd, c64s64, tw.reshape(128, 32 * 384), c256.reshape(128, 1024)


def _layout_weights(inp):
    f = np.float32
    w_in = inp["w_in"]
    kr = w_in[:, :, 1664:1696]
    w_in_ext = np.concatenate([w_in[:, :, :1696], kr[:, :, _PERM], w_in[:, :, 1696:]], axis=2).astype(f)
    w_uq = inp["w_uq"].reshape(DEPTH, 256, 8, 96)
    w_uq_ext = np.concatenate([w_uq[:, :, :, 64:96], w_uq[:, :, :, 64 + _PERM], w_uq[:, :, :, 0:64]], axis=3).reshape(DEPTH, 256, 1024).astype(f)
    wsT = np.ascontiguousarray(np.transpose(inp["g_ws"], (0, 1, 3, 2))).astype(f)
    vecs = np.zeros((DEPTH, 128, NV), f)
    for l in range(DEPTH):
        vecs[l, :, V_NG:V_NG + 8] = inp["norm_g"][l].reshape(8, 128).T
        vecs[l, :, V_BM:V_BM + 24] = inp["b_mod"][l].reshape(24, 128).T
        vecs[l, :, V_QAG:V_QAG + 2] = inp["q_a_g"][l].reshape(2, 128).T
        vecs[l, :, V_KVAG] = inp["kv_a_g"][l]
        qg = inp["q_norm_g"][l]
        kg = inp["k_norm_g"][l]
        vecs[l, :, V_GQ] = np.concatenate([qg[64:96], qg[64 + _PERM], qg[0:64]])
        vecs[l, 0:64, V_GKN] = kg[0:64]
        vecs[l, 64:128, V_GKN] = kg[0:64]
        vecs[l, 0:64, V_GKE] = np.concatenate([kg[64:96], kg[64 + _PERM]])
    glnbc = np.ascontiguousarray(np.broadcast_to(np.tile(inp["g_ln_g"], (1, 4))[:, None, :], (DEPTH, 128, 256))).astype(f)
    bs = inp["g_bs"]
    bsbc = np.ascontiguousarray(np.broadcast_to(bs[:, :, None, :], (DEPTH, 4, 64, 128))).reshape(DEPTH, 2, 128, 128)
    bsbc = np.ascontiguousarray(np.transpose(bsbc, (0, 2, 1, 3))).reshape(DEPTH, 128, 256).astype(f)
    return dict(w_in_ext=w_in_ext, w_uq_ext=w_uq_ext, wsT=wsT, vecs=vecs, glnbc=glnbc, bsbc=bsbc)


def build(depth=DEPTH, dbg=None, phases=4, sub=99):
    KSC = -0.5 * float(np.log(96.0))
    import os
    DBG_L = int(os.environ.get('DBG_L', '0'))
    nc = bass.Bass("TRN2", target_bir_lowering=False)
    P = Prog(nc)

    def din(name, shape, dt=F32):
        return nc.dram_tensor(name, list(shape), dt, kind="ExternalInput").ap()

    xin = din("xin", [T, D])
    cvec = din("cvec", [2, D])
    w_mod = din("w_mod", [DEPTH, D, 3 * D])
    w_in = din("w_in_ext", [DEPTH, D, 2240])
    w_fmix = din("w_fmix", [DEPTH, 256, 256])
    wsT_d = din("wsT", [DEPTH, 4, 128, 128])
    w_uq = din("w_uq_ext", [DEPTH, 256, 1024])
    w_ukv = din("w_ukv", [DEPTH, 128, 1024])
    w_out = din("w_out", [DEPTH, D, D])
    vecs_d = din("vecs", [DEPTH, 128, NV])
    glnbc_d = din("glnbc", [DEPTH, 128, 256])
    bsbc_d = din("bsbc", [DEPTH, 128, 256])
    tab_d = din("tab", [64, T])
    csbd_d = din("csbd", [128, 256])
    c64_d = din("c64s64", [128, 256])
    tw_d = din("tw", [128, 32 * 384])
    c256_d = din("c256", [128, 1024])
    out = nc.dram_tensor("out", [NX, D], F32, kind="ExternalOutput").ap()
    bufA = nc.dram_tensor("bufA", [T, D], F32, kind="Internal").ap()
    bufB = nc.dram_tensor("bufB", [T, D], F32, kind="Internal").ap()
    Kscr = nc.dram_tensor("Kscr", [8, 96, T], BF16, kind="Internal").ap()
    Vscr = nc.dram_tensor("Vscr", [8, 128, NT * 128], BF16, kind="Internal").ap()
    gscr = nc.dram_tensor("gscr", [2, D], F32, kind="Internal").ap()
    dbg_out = {}
    if dbg:
        for name, shape in dbg.items():
            dbg_out[name] = nc.dram_tensor("dbg_" + name, list(shape), F32, kind="ExternalOutput").ap()

    def sb(name, shape, dt=F32):
        return nc.alloc_sbuf_tensor("sb_" + name, list(shape), dt)

    ident = sb("ident", [128, 128], BF16)
    ones_bf = sb("ones_bf", [128, 128], BF16)
    onesq = sb("onesq", [128, 128], BF16)
    tab = sb("tab", [64, T], BF16)
    csbd = sb("csbd", [128, 256], BF16)
    c64 = sb("c64", [128, 256], BF16)
    c256 = sb("c256", [128, 2, 512], BF16)
    sk = sb("sk", [128, NT, 8])
    gate_bc = sb("gate_bc", [128, 2, D])
    sc = sb("sc", [128, 8, 2])
    scb = sb("scb", [128, 8, 2], BF16)
    cv = sb("cv", [128, 8, 2])
    modv = sb("modv", [128, 24, 2])
    amod = sb("amod", [128, 8, 2])
    vecs = sb("vecs", [128, NV])
    glnbc = sb("glnbc", [128, 256])
    bsbc = sb("bsbc", [128, 2, 128])
    UT = sb("UT", [128, 2, T], BF16)
    w_in_sb = sb("w_in_sb", [128, 8, 2240], BF16)
    w_out_sb = sb("w_out_sb", [128, 8, D], BF16)
    wq_sb = sb("wq_sb", [128, 2, 1024], BF16)
    wukv_sb = sb("wukv_sb", [128, 1024], BF16)
    wfm_sb = sb("wfm_sb", [128, 2, 256], BF16)
    wsT_sb = sb("wsT_sb", [128, 4, 128], BF16)
    xt = sb("xt", [128, 4, D])
    tq1 = sb("tq1", [128, 512])
    rbq1 = sb("rbq1", [128, 512])
    xs = [sb("xs%d" % i, [128, D], BF16) for i in range(2)]
    ssq = sb("ssq", [128, 4])
    rstd = sb("rstd", [128, 4])
    hT = sb("hT", [128, 8, 512], BF16)
    tmpA = sb("tmpA", [128, 512])
    tmpB = sb("tmpB", [128, 512])
    tmpC = sb("tmpC", [128, 512])
    tmpD = sb("tmpD", [128, 512])
    sqb = [sb("sqb%d" % i, [128, 512], BF16) for i in range(2)]
    rbc = sb("rbc", [128, 512])

    ARENA = 62 * 1024
    arena_t = sb("arena", [128, ARENA // 4])
    aoff = [0]

    def areset():
        aoff[0] = 0

    def carve(shape, dt=F32):
        esz = 4 if dt == F32 else 2
        n = 1
        for d_ in shape[1:]:
            n *= d_
        nbytes = (n * esz + 31) // 32 * 32
        assert aoff[0] + nbytes <= ARENA, ("arena overflow", aoff[0], nbytes)
        a = arena_t[0:shape[0], aoff[0] // 4:(aoff[0] + nbytes) // 4]
        aoff[0] += nbytes
        if dt == BF16:
            a = a.bitcast(BF16)
        a = a[:, 0:n]
        if len(shape) == 3:
            a = a.rearrange("p (a b) -> p a b", a=shape[1])
        elif len(shape) == 4:
            a = a.rearrange("p (a b c) -> p a b c", a=shape[1], b=shape[2])
        assert tuple(a.shape) == tuple(shape), (a.shape, shape)
        return a

    areset()
    wmb = carve([128, 8, 3 * D], BF16)
    areset()
    ckvn = carve([128, 512], BF16)
    krsq = carve([32, 512], BF16)
    Kst = carve([96, 8, 512], BF16)
    Vst = carve([128, 4, 8, 128], BF16)
    sqk = carve([128, 4, 64])
    ssqn = carve([128, 8])
    ssqr = carve([128, 8])
    xsq_p1 = carve([128, D])
    areset()
    TW = carve([128, 32, 384], BF16)
    ZT = carve([128, 2, 2, NX], BF16)
    ABs = [carve([128, 512], BF16) for i in range(2)]
    Vfs = [carve([128, 512], BF16) for i in range(2)]
    FTc = carve([128, 2, 256], BF16)
    areset()
    cqn = carve([128, 2, 512], BF16)
    sag = carve([128, 4, 512], BF16)
    ug = carve([128, 2, 512])
    QT = carve([96, 8, 512], BF16)
    NKH = 9
    Kb = [carve([96, NKH * 128], BF16) for i in range(2)]
    Vb = [carve([128, NKH, 128], BF16) for i in range(2)]
    PT = [carve([128, 512], BF16) for i in range(3)]
    mixT = carve([128, 8, 512], BF16)
    ost = [carve([128, D]) for i in range(2)]
    rbs = carve([64, 512])
    otmp = carve([128, 512])
    vnp = carve([128, 4, 128], BF16)
    vtmp = carve([128, 256])
    gst = carve([128, 16])
    gt = carve([128, 128])
    xsq_p3 = carve([128, D])
    tc1 = carve([32, 512])
    print("sbuf bytes remaining", nc.sbuf_bytes_remaining)

    psA = nc.alloc_psum_tensor("psA", [128, 512], F32)
    psB = nc.alloc_psum_tensor("psB", [128, 512], F32)
    psT = nc.alloc_psum_tensor("psT", [128, 1024], BF16)
    psS = [nc.alloc_psum_tensor("psS%d" % i, [128, 512], F32) for i in range(2)]
    psO = [nc.alloc_psum_tensor("psO%d" % i, [128, 512], F32) for i in range(2)]
    psM = nc.alloc_psum_tensor("psM", [128, 512], F32)
    P.excl = {"psA", "psB", "psT", "psS0", "psS1", "psO0", "psO1", "psM"}

    def mm(out_, lhsT, rhs, start, stop, r, w):
        P.add("pe", lambda e: e.matmul(out_, lhsT, rhs, start=start, stop=stop), reads=r, writes=w)

    def tr(out_, in_, r, w):
        P.add("pe", lambda e: e.transpose(out_, in_, ident[:]), reads=list(r) + ["ident"], writes=w)

    def act(out_, in_, func, r, w, scale=1.0):
        P.add("act", lambda e: e.activation(out=out_, in_=in_, func=func, scale=scale), reads=r, writes=w)

    def ts(eng, out_, in0, s1, s2, op0, op1, r, w):
        if s2 is None:
            P.add(eng, lambda e: e.tensor_scalar(out=out_, in0=in0, scalar1=s1, scalar2=None, op0=op0), reads=r, writes=w)
        else:
            P.add(eng, lambda e: e.tensor_scalar(out=out_, in0=in0, scalar1=s1, scalar2=s2, op0=op0, op1=op1), reads=r, writes=w)

    def tt(eng, out_, in0, in1, op, r, w):
        P.add(eng, lambda e: e.tensor_tensor(out=out_, in0=in0, in1=in1, op=op), reads=r, writes=w)

    def stt(eng, out_, in0, scalar, in1, op0, op1, r, w):
        P.add(eng, lambda e: e.scalar_tensor_tensor(out=out_, in0=in0, scalar=scalar, in1=in1, op0=op0, op1=op1), reads=r, writes=w)

    def cp(eng, out_, in_, r, w):
        if eng == "act":
            P.add("act", lambda e: e.copy(out=out_, in_=in_), reads=r, writes=w)
        else:
            P.add(eng, lambda e: e.tensor_copy(out=out_, in_=in_), reads=r, writes=w)

    def red(out_, in_, r, w):
        P.add("dve", lambda e: e.tensor_reduce(out=out_, in_=in_, axis=AX.X, op=ALU.add), reads=r, writes=w)

    def recip(out_, in_, r, w):
        P.add("dve", lambda e: e.reciprocal(out=out_, in_=in_), reads=r, writes=w)

    def memset(eng, ap, val, w):
        P.add(eng, lambda e: e.memset(ap, val), writes=w)

    def dma(q, out_, in_, r, w, slow=False):
        if slow:
            P.add(q, lambda e: e.dma_start(out=out_, in_=in_, allow_slow_non_contiguous=True), reads=r, writes=w, dma=True)
        else:
            P.add(q, lambda e: e.dma_start(out=out_, in_=in_), reads=r, writes=w, dma=True)

    def rsqrt_ps(out_, psin, n, r, w):
        P.add("act", lambda e: e.activation(out=out_, in_=psin, func=AF.Ln, scale=1.0 / n, bias=EPS), reads=r, writes=w)
        act(out_, out_, AF.Exp, w, w, scale=-0.5)

    def rsqrt_to(out_, in_, n, r, w, width_key):
        act(out_, in_, AF.Ln, r, w, scale=1.0 / n)
        act(out_, out_, AF.Exp, w, w, scale=-0.5)

    memset("pool", ident[:], 0.0, ["ident"])
    P.add("pool", lambda e: e.affine_select(out=ident[:], in_=ident[:], compare_op=ALU.not_equal, fill=1.0, base=0,
                                            pattern=[[-1, 128]], channel_multiplier=1), reads=["ident"], writes=["ident"])
    memset("pool", ones_bf[:], 1.0, ["ones_bf"])
    memset("pool", onesq[:], 1.0, ["onesq"])
    memset("pool", onesq[32:64, :], 0.0, ["onesq"])
    dma("pool", tab[:], tab_d, [], ["tab"])
    dma("pool", csbd[:], csbd_d, [], ["csbd"])
    dma("pool", c64[:], c64_d, [], ["c64"])
    dma("pool", c256[:], c256_d.rearrange("p (t k) -> p t k", t=2), [], ["c256"])
    for jj in range(2):
        dma("sp", cv[:, :, jj], cvec[jj].rearrange("(k p) -> p k", p=128), [], ["cv"], slow=True)
    act(sc[:], cv[:], AF.Exp, ["cv"], ["sc"], scale=-1.0)
    ts("dve", sc[:], sc[:], 1.0, None, ALU.add, None, ["sc"], ["sc"])
    recip(sc[:], sc[:], ["sc"], ["sc"])
    tt("dve", sc[:], sc[:], cv[:], ALU.mult, ["sc", "cv"], ["sc"])
    cp("dve", scb[:], sc[:], ["sc"], ["scb"])

    bufs = [bufA, bufB]

    def load_x(src, t0, w, tis=None):
        nt = w // 128
        for ti in (range(nt) if tis is None else tis):
            if ti < nt:
                dma("sp", xt[:, ti, :], src[t0 + ti * 128:t0 + (ti + 1) * 128, :], [("xy", id(src), (t0 // 128) + ti)], [("xt", ti)])

    def build_hT(src, t0, w, j, xsq, loaded=False):
        nt = w // 128
        if not loaded:
            load_x(src, t0, w)
        for ti in range(nt):
            act(xsq[:], xt[:, ti, :], AF.Square, [("xt", ti)], ["xsq"])
            red(ssq[:, ti:ti + 1], xsq[:], ["xsq"], ["ssq"])
        ts("dve", ssq[:, 0:nt], ssq[:, 0:nt], D * EPS, None, ALU.add, None, ["ssq"], ["ssq"])
        rsqrt_to(rstd[:, 0:nt], ssq[:, 0:nt], D, ["ssq"], ["rstd"], None)
        for ti in range(nt):
            xsb = xs[ti % 2]
            ts("dve", xsb[:], xt[:, ti, :], rstd[:, ti:ti + 1], None, ALU.mult, None, [("xt", ti), "rstd"], [("xs", ti % 2)])
            for k in range(8):
                tr(psT[:, k * 128:(k + 1) * 128], xsb[:, k * 128:(k + 1) * 128], [("xs", ti % 2)], ["psT"])
            for k in range(8):
                ts("dve", hT[:, k, ti * 128:(ti + 1) * 128], psT[:, k * 128:(k + 1) * 128],
                   amod[:, k, j:j + 1], modv[:, k, j:j + 1], ALU.mult, ALU.add, ["psT", "amod", "modv"], ["hT"])

    def key(ps):
        return ps.name

    def proj_fm(ps, col0, ncol, w, wkey="w_in"):
        for k in range(8):
            mm(ps[0:ncol, 0:w], w_in_sb[:, k, col0:col0 + ncol], hT[:, k, 0:w], k == 0, k == 7, ["hT", (wkey, k)], [key(ps)])

    blocks = [(0, 256, 1)] + [(NC_ + 512 * i, 512, 0) for i in range(8)]

    for l in range(depth):
        src = xin if l == 0 else bufs[(l - 1) % 2]
        last = (l == depth - 1)
        dst = out if last else bufs[l % 2]
        if phases < 1:
            break
        P.barrier()
        for k in range(8):
            dma("pool", w_in_sb[:, k, :], w_in[l, k * 128:(k + 1) * 128, :], [], [("w_in", k)])
        for k in range(8):
            dma("pool", w_out_sb[:, k, :], w_out[l, k * 128:(k + 1) * 128, :], [], [("w_out", k)])
        for k in range(2):
            dma("pool", wq_sb[:, k, :], w_uq[l, k * 128:(k + 1) * 128, :], [], ["wq"])
            dma("pool", wfm_sb[:, k, :], w_fmix[l, k * 128:(k + 1) * 128, :], [], ["wfm"])
        dma("pool", wukv_sb[:], w_ukv[l], [], ["wukv"])
        for h in range(4):
            dma("pool", wsT_sb[:, h, :], wsT_d[l, h], [], ["wsT"])
        dma("sp", vecs[:], vecs_d[l], [], ["vecs"])
        dma("sp", glnbc[:], glnbc_d[l], [], ["glnbc"])
        dma("sp", bsbc[:], bsbc_d[l].rearrange("p (a q) -> p a q", a=2), [], ["bsbc"])
        for k in range(8):
            dma("pool", wmb[:, k, :], w_mod[l, k * 128:(k + 1) * 128, :], [], [("wmb", k)])
        for f in range(24):
            for k in range(8):
                mm(psM[:, 2 * f:2 * f + 2], wmb[:, k, f * 128:(f + 1) * 128], scb[:, k, :], k == 0, k == 7, [("wmb", k), "scb"], ["psM"])
        pm = psM[:, 0:48].rearrange("p (f j) -> p f j", j=2)
        for jj in range(2):
            tt("dve", modv[:, :, jj], pm[:, :, jj], vecs[:, V_BM:V_BM + 24], ALU.add, ["psM", "vecs"], ["modv"])
        for jj in range(2):
            stt("dve", amod[:, :, jj], modv[:, 8:16, jj], 1.0, vecs[:, V_NG:V_NG + 8], ALU.add, ALU.mult, ["modv", "vecs"], ["amod"])
        for jj in range(2):
            dma("sp", gscr[jj].rearrange("(k p) -> p k", p=128), modv[:, 16:24, jj], ["modv"], ["gscr"], slow=True)
        for jj in range(2):
            dma("sp", gate_bc[:, jj, :], gscr[jj].partition_broadcast(128), ["gscr"], ["gate_bc"])

        if phases < 2:
            break
        P.barrier()
        memset("pool", Vst[:, :, :, 64:128], 1.0, ["Vst"])
        for (t0, w, j) in blocks:
            nt = w // 128
            tt0 = t0 // 128
            bidx = blocks.index((t0, w, j))
            build_hT(src, t0, w, j, xsq_p1, loaded=(bidx > 0))
            if bidx + 1 < len(blocks):
                load_x(src, blocks[bidx + 1][0], blocks[bidx + 1][1])
            if dbg and "hTp1" in dbg_out and l == DBG_L and t0 == NC_:
                dma("pool", dbg_out["hTp1"].rearrange("p (a b) -> p a b", a=8), hT[:, :, :], ["hT"], ["dbghTp1"])
                dma("pool", dbg_out["win"], w_in_sb[:, :, 0:256], ["w_in"], ["dbgwin"])
            if sub < 2:
                break
            for cc in range(2):
                ps = psA if cc == 0 else psB
                proj_fm(ps, cc * 128, 128, w)
                cp("act" if cc == 0 else "dve", UT[:, cc, t0:t0 + w], ps[:, 0:w], [key(ps)], ["UT"])
            if sub < 3:
                break
            proj_fm(psA, 1536, 128, w)
            act(sqb[0][:, 0:w], psA[:, 0:w], AF.Square, ["psA"], [("sqb", 0)])
            mm(psM[:, 0:w], ones_bf[:], sqb[0][:, 0:w], True, True, [("sqb", 0), "ones_bf"], ["psM"])
            ts("dve", rbc[:, 0:w], psM[:, 0:w], 128 * EPS, None, ALU.add, None, ["psM"], ["rbc"])
            rsqrt_to(rbc[:, 0:w], rbc[:, 0:w], 128, ["rbc"], ["rbc"], None)
            stt("dve", ckvn[:, 0:w], psA[:, 0:w], vecs[:, V_KVAG:V_KVAG + 1], rbc[:, 0:w], ALU.mult, ALU.mult, ["psA", "vecs", "rbc"], ["ckvn"])
            if sub < 4:
                break
            proj_fm(psB, 1664, 64, w)
            act(krsq[:, 0:w], psB[0:32, 0:w], AF.Square, ["psB"], ["krsq"])
            stt("dve", tmpA[0:64, 0:w], psB[0:64, 0:w], vecs[0:64, V_GKE:V_GKE + 1], tab[:, t0:t0 + w], ALU.mult, ALU.mult,
                ["psB", "vecs", "tab"], ["tmpA"])
            cp("dve", tmpB[0:32, 0:w], tmpA[32:64, 0:w], ["tmpA"], ["tmpB"])
            tt("dve", tmpA[64:96, 0:w], tmpA[0:32, 0:w], tmpB[0:32, 0:w], ALU.add, ["tmpA", "tmpB"], ["tmpA"])
            if sub < 5:
                break
            for h in range(8):
                hb = h % 2
                ps = psA if hb == 0 else psB
                rq = rbc if hb == 0 else rbq1
                rqk = "rbc" if hb == 0 else "rbq1"
                mm(ps[0:64, 0:w], wukv_sb[:, h * 128:h * 128 + 64], ckvn[:, 0:w], True, True, ["wukv", "ckvn"], [key(ps)])
                act(sqb[hb][0:64, 0:w], ps[0:64, 0:w], AF.Square, [key(ps)], [("sqb", hb)])
                mm(psM[0:96, 0:w], ones_bf[0:64, 0:96], sqb[hb][0:64, 0:w], True, False, [("sqb", hb), "ones_bf"], ["psM"])
                mm(psM[0:96, 0:w], ones_bf[0:32, 0:96], krsq[0:32, 0:w], False, True, ["krsq", "ones_bf"], ["psM"])
                P.add("act", (lambda o_, i_: (lambda e: e.activation(out=o_, in_=i_, func=AF.Ln, scale=1.0 / 96, bias=EPS)))(rq[0:96, 0:w], psM[0:96, 0:w]),
                      reads=["psM"], writes=[rqk])
                P.add("act", (lambda o_: (lambda e: e.activation(out=o_, in_=o_, func=AF.Exp, scale=-0.5, bias=KSC)))(rq[0:96, 0:w]),
                      reads=[rqk], writes=[rqk])
                stt("dve", Kst[0:64, h, 0:w], ps[0:64, 0:w], vecs[0:64, V_GKN:V_GKN + 1], rq[0:64, 0:w], ALU.mult, ALU.mult,
                    [key(ps), "vecs", rqk], [("Kst", "n")])
                tt("dve", Kst[64:96, h, 0:w], tmpA[64:96, 0:w], rq[64:96, 0:w], ALU.mult, ["tmpA", rqk], [("Kst", "r")])
            if sub < 6:
                break
            for h in range(8):
                dma("sp", Kscr[h, :, t0:t0 + w], Kst[:, h, 0:w], [("Kst", "n"), ("Kst", "r")], ["Kscr"])
            if sub < 7:
                break
            import os
            CUT = int(os.environ.get("CUT", "99"))
            for ti in range(nt):
                for half in range(2):
                    ps = psA if half == 0 else psB
                    if CUT >= 1:
                        mm(ps[:, 0:512], ckvn[:, ti * 128:(ti + 1) * 128], wukv_sb[:, half * 512:(half + 1) * 512], True, True, ["ckvn", "wukv"], [key(ps)])
                    pv = ps[:, 0:512].rearrange("p (h c) -> p h c", c=128)
                    if CUT >= 2:
                        cp("dve", Vst[:, ti, half * 4:half * 4 + 4, 0:64], pv[:, :, 64:128], [key(ps)], ["Vst"])
            if sub < 8:
                break
            for h in range(8):
                dma("sp", Vscr[h].rearrange("p (t c) -> p t c", c=128)[:, tt0:tt0 + nt, :], Vst[:, 0:nt, h, :], ["Vst"], ["Vscr"])

        if phases < 3:
            break
        P.barrier()
        dma("pool", TW[:], tw_d.rearrange("p (j c) -> p j c", c=384), [], ["TW"])
        if dbg and "UT" in dbg_out and l == DBG_L:
            dma("pool", dbg_out["UT"].rearrange("p (a b) -> p a b", a=2), UT[:, :, :], ["UT"], ["dbgUT"])
            dma("sp", dbg_out["sk"], sk[:].rearrange("p a b -> p (a b)"), ["sk"], ["dbgsk"])
        for t in range(2):
            for cc in range(2):
                mm(psA[:, cc * 256:(cc + 1) * 256], UT[:, cc, t * 128:(t + 1) * 128], csbd[:], True, True, ["UT", "csbd"], ["psA"])
            cp("dve", ABs[t][:], psA[:], ["psA"], [("ABs", t)])
        for cc in range(2):
            for t in range(2):
                mm(psB[:, cc * 256:(cc + 1) * 256], ABs[t][:, cc * 256:cc * 256 + 128], c256[:, t, 0:256], t == 0, False, [("ABs", t), "c256"], ["psB"])
                mm(psB[:, cc * 256:(cc + 1) * 256], ABs[t][:, cc * 256 + 128:cc * 256 + 256], c256[:, t, 256:512], False, t == 1, [("ABs", t), "c256"], ["psB"])
        cp("dve", FTc[:], psB[:].rearrange("p (a k) -> p a k", a=2), ["psB"], ["FTc"])
        for c2 in range(2):
            for cc in range(2):
                mm(psA[:, c2 * 256:(c2 + 1) * 256], wfm_sb[:, cc, c2 * 128:(c2 + 1) * 128], FTc[:, cc, :], cc == 0, cc == 1, ["wfm", "FTc"], ["psA"])
        cp("dve", UT[:, :, 0:256], psA[:].rearrange("p (a k) -> p a k", a=2), ["psA"], ["UT"])
        for jq in range(32):
            p1, p2 = (psA, psB) if jq % 2 == 0 else (psS[0], psS[1])
            ab = ABs[jq % 2]
            for cc in range(2):
                for clo in range(2):
                    c = 2 * jq + clo
                    mm(p1[clo * 64:(clo + 1) * 64, cc * 256:(cc + 1) * 256], UT[:, cc, NC_ + c:NC_ + NX:64], csbd[:], True, True, ["UT", "csbd"], [key(p1)])
            cp("act" if jq % 2 == 0 else "dve", ab[:], p1[:], [key(p1)], [("ABs", jq % 2)])
            for cc in range(2):
                mm(p2[:, cc * 256:(cc + 1) * 256], ab[:, cc * 256:cc * 256 + 128], TW[:, jq, 128:384], True, False, [("ABs", jq % 2), "TW"], [key(p2)])
                mm(p2[:, cc * 256:(cc + 1) * 256], ab[:, cc * 256 + 128:cc * 256 + 256], TW[:, jq, 0:256], False, True, [("ABs", jq % 2), "TW"], [key(p2)])
            cp("dve" if jq % 2 == 0 else "act", ZT[:, :, :, jq * 128:(jq + 1) * 128], p2[:].rearrange("p (a b k) -> p a b k", a=2, b=2), [key(p2)], ["ZT"])
        for iq in range(32):
            p1, p2 = (psA, psB) if iq % 2 == 0 else (psS[0], psS[1])
            vf = Vfs[iq % 2]
            for plo in range(2):
                pp = 2 * iq + plo
                for ri in range(2):
                    for cc in range(2):
                        mm(p1[plo * 64:(plo + 1) * 64, ri * 256:(ri + 1) * 256], ZT[:, cc, ri, pp:NX:64], wfm_sb[:, cc, :], cc == 0, cc == 1, ["ZT", "wfm"], [key(p1)])
            cp("act" if iq % 2 == 0 else "dve", vf[:], p1[:], [key(p1)], [("Vfs", iq % 2)])
            for c2 in range(2):
                mm(p2[:, c2 * 128:(c2 + 1) * 128], vf[:, c2 * 128:(c2 + 1) * 128], c64[:, 0:128], True, False, [("Vfs", iq % 2), "c64"], [key(p2)])
                mm(p2[:, c2 * 128:(c2 + 1) * 128], vf[:, 256 + c2 * 128:256 + (c2 + 1) * 128], c64[:, 128:256], False, True, [("Vfs", iq % 2), "c64"], [key(p2)])
            for c2 in range(2):
                ov = UT[:, c2, NC_:T].rearrange("p (q r) -> p r q", r=64)[:, 2 * iq:2 * iq + 2, :]
                cp("dve" if iq % 2 == 0 else "act", ov, p2[:, c2 * 128:(c2 + 1) * 128].rearrange("p (a q) -> p a q", a=2), [key(p2)], ["UT"])

        if phases < 4:
            break
        P.barrier()
        memset("pool", vnp[:], 0.0, ["vnp"])
        import os
        CUT3 = int(os.environ.get('CUT3', '99'))
        p3blocks = blocks[1:] if last else blocks
        p3blocks = p3blocks[:int(os.environ.get('NB3', '99'))]
        chunks = []
        for bi, (t0, w, j) in enumerate(p3blocks):
            for h in range(8):
                if j == 1:
                    chunks.append((bi, h, 0, 2))
                else:
                    for k0_ in range(0, NT, NKH):
                        chunks.append((bi, h, k0_, min(NT, k0_ + NKH)))
        nchunk = len(chunks)

        def issue_chunk(ci):
            bi_, h_, k0, k1 = chunks[ci]
            s = ci % 2
            nk = k1 - k0
            dma("sp", Kb[s][:, 0:nk * 128], Kscr[h_, :, k0 * 128:k1 * 128], ["Kscr"], [("Kb", s)])
            dma("sp", Vb[s][:, 0:nk, :], Vscr[h_].rearrange("p (t c) -> p t c", c=128)[:, k0:k1, :], ["Vscr"], [("Vb", s)])

        issue_chunk(0)
        if nchunk > 1:
            issue_chunk(1)
        ci = 0
        for bi, (t0, w, j) in enumerate(p3blocks):
            nt = w // 128
            tt0 = t0 // 128
            build_hT(src, t0, w, j, xsq_p3, loaded=(bi > 0))

            def silu_r(ps, r_out, rk):
                act(r_out, ps[:, 0:w], AF.Exp, [key(ps)], [rk], scale=-1.0)
                P.add("act", lambda e: e.activation(out=r_out, in_=r_out, func=AF.Ln, bias=1.0), reads=[rk], writes=[rk])
                act(r_out, r_out, AF.Exp, [rk], [rk], scale=-1.0)

            if CUT3 < 2:
                break
            for cc in range(2):
                ps = psA if cc == 0 else psB
                proj_fm(ps, 256 + cc * 128, 128, w)
                r_ = tmpA[:, 0:w] if cc == 0 else tmpB[:, 0:w]
                rk = "tmpA" if cc == 0 else "tmpB"
                silu_r(ps, r_, rk)
                tt("dve", r_, ps[:, 0:w], r_, ALU.mult, [key(ps), rk], [rk])
                tt("dve", mixT[:, cc, 0:w], r_, UT[:, cc, t0:t0 + w], ALU.mult, [rk, "UT"], [("mixT", cc)])
            if CUT3 < 3:
                break
            for cc in range(2):
                pu, pg = (psS[0], psS[1]) if cc == 0 else (psA, psB)
                proj_fm(pu, 512 + cc * 128, 128, w)
                proj_fm(pg, 1024 + cc * 128, 128, w)
                r_ = tmpC[:, 0:w]
                silu_r(pg, r_, "tmpC")
                tt("dve", r_, pg[:, 0:w], r_, ALU.mult, [key(pg), "tmpC"], ["tmpC"])
                tt("dve", ug[:, cc, 0:w], pu[:, 0:w], r_, ALU.mult, [key(pu), "tmpC"], ["ug"])
            if CUT3 < 4:
                break
            proj_fm(psS[0], 1280, 128, w)
            proj_fm(psS[1], 1408, 128, w)
            act(sqb[0][:, 0:w], psS[0][:, 0:w], AF.Square, ["psS0"], [("sqb", 0)])
            act(sqb[1][:, 0:w], psS[1][:, 0:w], AF.Square, ["psS1"], [("sqb", 1)])
            mm(psM[:, 0:w], ones_bf[:], sqb[0][:, 0:w], True, False, [("sqb", 0), "ones_bf"], ["psM"])
            mm(psM[:, 0:w], ones_bf[:], sqb[1][:, 0:w], False, True, [("sqb", 1), "ones_bf"], ["psM"])
            ts("dve", rbc[:, 0:w], psM[:, 0:w], 256 * EPS, None, ALU.add, None, ["psM"], ["rbc"])
            rsqrt_to(rbc[:, 0:w], rbc[:, 0:w], 256, ["rbc"], ["rbc"], None)
            stt("dve", cqn[:, 0, 0:w], psS[0][:, 0:w], vecs[:, V_QAG:V_QAG + 1], rbc[:, 0:w], ALU.mult, ALU.mult, ["psS0", "vecs", "rbc"], ["cqn"])
            stt("dve", cqn[:, 1, 0:w], psS[1][:, 0:w], vecs[:, V_QAG + 1:V_QAG + 2], rbc[:, 0:w], ALU.mult, ALU.mult, ["psS1", "vecs", "rbc"], ["cqn"])
            if CUT3 < 5:
                break
            for ac in range(4):
                ps = (psA, psB, psS[0], psS[1])[ac]
                proj_fm(ps, 1728 + ac * 128, 128, w)
                r_ = tmpA[:, 0:w] if ac % 2 == 0 else tmpB[:, 0:w]
                rk = "tmpA" if ac % 2 == 0 else "tmpB"
                silu_r(ps, r_, rk)
                tt("dve", sag[:, ac, 0:w], ps[:, 0:w], r_, ALU.mult, [key(ps), rk], ["sag"])
            if CUT3 < 6:
                break
            for ti in range(nt):
                tsl = slice(ti * 128, (ti + 1) * 128)
                for k in range(8):
                    mm(psA[:, 0:256], hT[:, k, tsl], w_in_sb[:, k, 768:1024], k == 0, k == 7, ["hT", ("w_in", k)], ["psA"])
                pv = psA[:, 0:256].rearrange("p (h c) -> p h c", c=64)
                red(gst[:, 0:4], pv, ["psA"], ["gst"])
                act(vtmp[:], psA[:, 0:256], AF.Square, ["psA"], ["vtmp"])
                red(gst[:, 4:8], vtmp[:].rearrange("p (h c) -> p h c", c=64), ["vtmp"], ["gst"])
                ts("dve", gst[:, 0:4], gst[:, 0:4], 1.0 / 64, None, ALU.mult, None, ["gst"], ["gst"])
                tt("dve", gst[:, 8:12], gst[:, 0:4], gst[:, 0:4], ALU.mult, ["gst"], ["gst"])
                stt("dve", gst[:, 4:8], gst[:, 4:8], 1.0 / 64, gst[:, 8:12], ALU.mult, ALU.subtract, ["gst"], ["gst"])
                ts("dve", gst[:, 4:8], gst[:, 4:8], EPS, None, ALU.add, None, ["gst"], ["gst"])
                rsqrt_to(gst[:, 12:16], gst[:, 4:8], 1.0, ["gst"], ["gst"], None)
                for h in range(4):
                    ts("dve", vtmp[:, h * 64:(h + 1) * 64], psA[:, h * 64:(h + 1) * 64], gst[:, h:h + 1], gst[:, 12 + h:13 + h],
                       ALU.subtract, ALU.mult, ["psA", "gst"], ["vtmp"])
                vo = vnp[:].rearrange("p (a b) c -> p a b c", b=2)
                for par in range(2):
                    tt("pool", vo[:, :, par, par * 64:par * 64 + 64],
                       vtmp[:].rearrange("p (a b c) -> p a b c", b=2, c=64)[:, :, par, :],
                       glnbc[:].rearrange("p (a b c) -> p a b c", b=2, c=64)[:, :, par, :], ALU.mult, ["vtmp", "glnbc"], ["vnp"])
                for pr in range(2):
                    ps = psB
                    for par in range(2):
                        h = 2 * pr + par
                        mm(ps[:, pr * 128:(pr + 1) * 128], vnp[:, h, :], wsT_sb[:, h, :], par == 0, par == 1, ["vnp", "wsT"], ["psB"])
                for pr in range(2):
                    tt("dve", gt[:], psB[:, pr * 128:(pr + 1) * 128], bsbc[:, pr, :], ALU.add, ["psB", "bsbc"], ["gt"])
                    tt("pool", mixT[:, 2 + pr, tsl], gt[:], ug[:, pr, tsl], ALU.mult, ["gt", "ug"], [("mixT", 2 + pr)])
            if CUT3 < 7:
                break
            for h in range(8):
                hb = h % 2
                ps = psA if hb == 0 else psB
                sq_ = sqb[hb]
                tq = tmpD if hb == 0 else tq1
                rq = rbc if hb == 0 else rbq1
                tcb = tmpC if hb == 0 else tc1
                tqk, rqk, tck = ("tmpD", "rbc", "tmpC") if hb == 0 else ("tq1", "rbq1", "tc1")
                for kc in range(2):
                    mm(ps[:, 0:w], wq_sb[:, kc, h * 128:(h + 1) * 128], cqn[:, kc, 0:w], kc == 0, kc == 1, ["wq", "cqn"], [key(ps)])
                act(sq_[:, 0:w], ps[:, 0:w], AF.Square, [key(ps)], [("sqb", hb)])
                mm(psM[:, 0:w], onesq[:], sq_[:, 0:w], True, True, [("sqb", hb), "onesq"], ["psM"])
                rsqrt_ps(rq[:, 0:w], psM[:, 0:w], 96, ["psM"], [rqk])
                stt("dve", tq[:, 0:w], ps[:, 0:w], vecs[:, V_GQ:V_GQ + 1], rq[:, 0:w], ALU.mult, ALU.mult, [key(ps), "vecs", rqk], [tqk])
                cp("dve", QT[0:64, h, 0:w], tq[64:128, 0:w], [tqk], [("QT", h)])
                tt("dve", tq[0:64, 0:w], tq[0:64, 0:w], tab[:, t0:t0 + w], ALU.mult, [tqk, "tab"], [tqk])
                cp("dve", tcb[0:32, 0:w], tq[32:64, 0:w], [tqk], [tck])
                tt("dve", QT[64:96, h, 0:w], tq[0:32, 0:w], tcb[0:32, 0:w], ALU.add, [tqk, tck], [("QT", h)])
            if CUT3 < 8:
                break

            def epilogue(h_, po):
                cp("dve", rbs[:, 0:w], tmpD[64:128, 0:w], ["tmpD"], ["rbs"])
                pa = (h_ % 2) * 64
                tt("dve", otmp[pa:pa + 64, 0:w], po[0:64, 0:w], rbs[:, 0:w], ALU.mult, [key(po), "rbs"], ["otmp"])
                tt("dve", mixT[pa:pa + 64, 4 + h_ // 2, 0:w], otmp[pa:pa + 64, 0:w], sag[pa:pa + 64, h_ // 2, 0:w], ALU.mult,
                   ["otmp", "sag"], [("mixT", 4 + h_ // 2, h_ % 2)])

            tiles = []
            cj = ci
            for h in range(8):
                nsub = 1 if j == 1 else (NT + NKH - 1) // NKH
                tot_kt = 2 if j == 1 else NT
                n_ = 0
                for sbi in range(nsub):
                    _, _, k0, k1 = chunks[cj]
                    for kt in range(k0, k1):
                        tiles.append((h, kt, cj, kt - k0, n_ == 0, n_ == tot_kt - 1, kt == k1 - 1))
                        n_ += 1
                    cj += 1
            ci = cj

            def emit_qk(i):
                h_, kt_, c_, kl_, _, _, _ = tiles[i]
                pss = psS[i % 2]
                mm(pss[:, 0:w], Kb[c_ % 2][:, kl_ * 128:(kl_ + 1) * 128], QT[:, h_, 0:w], True, True, [("Kb", c_ % 2), ("QT", h_)], [key(pss)])

            emit_qk(0)
            for i in range(len(tiles)):
                h_, kt_, c_, kl_, first_, last_, chunk_end = tiles[i]
                if i + 1 < len(tiles):
                    emit_qk(i + 1)
                pss = psS[i % 2]
                pt = PT[i % 3]
                po = psO[h_ % 2]
                act(pt[:, 0:w], pss[:, 0:w], AF.Exp, [key(pss)], [("PT", i % 3)])
                mm(po[:, 0:w], Vb[c_ % 2][:, kl_, :], pt[:, 0:w], first_, last_, [("Vb", c_ % 2), ("PT", i % 3)], [key(po)])
                if chunk_end and c_ + 2 < nchunk:
                    issue_chunk(c_ + 2)
                if last_:
                    recip(tmpD[64:128, 0:w], po[64:128, 0:w], [key(po)], ["tmpD"])
                    epilogue(h_, po)
            if dbg and "hT" in dbg_out and l == DBG_L and bi == 0:
                dma("pool", dbg_out["hT"].rearrange("p (a b) -> p a b", a=8), hT[:, :, :], ["hT"], ["dbghT"])
                dma("sp", dbg_out["modv"], modv[:].rearrange("p a b -> p (a b)"), ["modv"], ["dbgmodv"])
                dma("sp", dbg_out["amod"], amod[:].rearrange("p a b -> p (a b)"), ["amod"], ["dbgamod"])
            if dbg and "mix" in dbg_out and l == DBG_L and bi == 0:
                _mk = [("mixT", 0), ("mixT", 1), ("mixT", 2), ("mixT", 3)] + [("mixT", 4 + a, b) for a in range(4) for b in range(2)]
                dma("pool", dbg_out["mix"].rearrange("p (a b) -> p a b", a=8), mixT[:, :, :], _mk, ["dbgmix"])
            mixkeys = [("mixT", 0), ("mixT", 1), ("mixT", 2), ("mixT", 3)] + [("mixT", 4 + a, b) for a in range(4) for b in range(2)]
            for ti in range(nt):
                tsl = slice(ti * 128, (ti + 1) * 128)
                o_ = ost[ti % 2]
                for half in range(2):
                    ps = psA if half == 0 else psB
                    for kc in range(8):
                        mm(ps[:, 0:512], mixT[:, kc, tsl], w_out_sb[:, kc, half * 512:(half + 1) * 512], kc == 0, kc == 7, mixkeys + [("w_out", kc)], [key(ps)])
                    tmp_ = tmpA if half == 0 else tmpB
                    tk = "tmpA" if half == 0 else "tmpB"
                    tt("dve", tmp_[:], ps[:, 0:512], gate_bc[:, j, half * 512:(half + 1) * 512], ALU.mult, [key(ps), "gate_bc"], [tk])
                    tt("pool", o_[:, half * 512:(half + 1) * 512], tmp_[:], xt[:, ti, half * 512:(half + 1) * 512], ALU.add, [tk, ("xt", ti)], [("ost", ti % 2)])
                if bi + 1 < len(p3blocks):
                    load_x(src, p3blocks[bi + 1][0], p3blocks[bi + 1][1], tis=[ti] if ti < nt - 1 else list(range(ti, 4)))
                r0 = t0 + ti * 128
                if last:
                    dma("sp", dst[r0 - NC_:r0 - NC_ + 128, :], o_[:], [("ost", ti % 2)], [("xy", id(dst), tt0 + ti)])
                else:
                    dma("sp", dst[r0:r0 + 128, :], o_[:], [("ost", ti % 2)], [("xy", id(dst), tt0 + ti)])
    if dbg and "bufA" in dbg_out:
        P.barrier()
        dma("sp", dbg_out["bufA"], bufA, [], ["dbgbufA"])
    P.emit()
    return nc


_NC_CACHE = {}


def kernel(**inputs):
    inp = {k: np.asarray(v) for k, v in inputs.items()}
    if "nc" not in _NC_CACHE:
        _NC_CACHE["nc"] = build(DEPTH)
    nc = _NC_CACHE["nc"]
    tab, csbd, c64s64, tw, c256 = _consts()
    lw = _layout_weights(inp)
    shared = dict(
        w_mod=np.ascontiguousarray(inp["w_mod"], np.float32), w_in_ext=lw["w_in_ext"],
        w_fmix=np.ascontiguousarray(inp["w_fmix"], np.float32), wsT=lw["wsT"], w_uq_ext=lw["w_uq_ext"],
        w_ukv=np.ascontiguousarray(inp["w_ukv"], np.float32), w_out=np.ascontiguousarray(inp["w_out"], np.float32),
        vecs=lw["vecs"], glnbc=lw["glnbc"], bsbc=lw["bsbc"], tab=tab, csbd=csbd, c64s64=c64s64, tw=tw, c256=c256)
    in_maps = []
    for b in range(8):
        m = dict(shared)
        m["xin"] = np.ascontiguousarray(np.concatenate([inp["ctx"][b], inp["x"][b]], axis=0), np.float32)
        m["cvec"] = np.ascontiguousarray(np.stack([inp["c"][b], inp["c_ctx"]], axis=0), np.float32)
        in_maps.append(m)
    res = run_bass_kernel_spmd(nc, in_maps, core_ids=list(range(8)))
    return np.stack([np.asarray(r["out"], np.float32) for r in res.results], axis=0)
```
